# Optimizing a Trainium2 kernel written in Bass

```python
import jax, jax.numpy as jnp
from jax import lax
import numpy as np

D_MODEL = 1024
BATCH = 8
SEQ = 2048
DEPTH = 2

CHUNK = 64
Q_BLOCK = 128
A_HEADS = 8
A_HEAD_DIM = D_MODEL // 16
A_WIDTH = A_HEADS * A_HEAD_DIM
POOL_WINDOWS = (2, 4, 8, 16)
POOL_GROUPS = len(POOL_WINDOWS)
POOL_WIDTH = D_MODEL // 4
POOL_GROUP_DIM = POOL_WIDTH // POOL_GROUPS
CONV_WIDTH = D_MODEL // 4
CONV_K = 3
N_BRANCH = 3
MIX_WIDTH = A_WIDTH + POOL_WIDTH + CONV_WIDTH
D_FF = 4 * D_MODEL
RMS_EPS = 1e-6
NEG_INF = -1e30
IN_COLS = 3 * A_WIDTH + A_HEADS + POOL_WIDTH + 3 * CONV_WIDTH + N_BRANCH * D_MODEL

kernel_name = 'hybrid_fox_pool_conv_block'


def rmsnorm(x, g):
    xf = x.astype(jnp.float32)
    y = xf * lax.rsqrt(jnp.mean(xf * xf, axis=-1, keepdims=True) + RMS_EPS)
    return (y * g.astype(jnp.float32)).astype(x.dtype)


def forgetting_attention(q, k, v, f_logit):
    b, s, h, dh = q.shape
    q = q.transpose(0, 2, 1, 3)
    k = k.transpose(0, 2, 1, 3)
    v = v.transpose(0, 2, 1, 3)
    log_f = jax.nn.log_sigmoid(f_logit.astype(jnp.float32))
    cum_f = jnp.cumsum(log_f, axis=1).transpose(0, 2, 1)
    scale = dh ** -0.5
    outs = []
    for i in range(s // Q_BLOCK):
        qs, qe = i * Q_BLOCK, (i + 1) * Q_BLOCK
        qb = q[:, :, qs:qe]
        kb = k[:, :, :qe]
        vb = v[:, :, :qe]
        logits = jnp.einsum('bhqd,bhkd->bhqk', qb, kb).astype(jnp.float32) * scale
        logits = logits + cum_f[:, :, qs:qe, None] - cum_f[:, :, None, :qe]
        causal = jnp.arange(qs, qe)[:, None] >= jnp.arange(qe)[None, :]
        logits = jnp.where(causal[None, None], logits, NEG_INF)
        p = jax.nn.softmax(logits, axis=-1).astype(v.dtype)
        outs.append(jnp.einsum('bhqk,bhkd->bhqd', p, vb))
    o = jnp.concatenate(outs, axis=2)
    return o.transpose(0, 2, 1, 3).reshape(b, s, h * dh)


def pool_mixer(u, w_pool, pool_scale):
    b, s, _ = u.shape
    uf = u.astype(jnp.float32)
    cs = jnp.cumsum(uf, axis=1)
    groups = []
    for g, w in enumerate(POOL_WINDOWS):
        sl = slice(g * POOL_GROUP_DIM, (g + 1) * POOL_GROUP_DIM)
        cs_g = cs[..., sl]
        lagged = jnp.pad(cs_g, ((0, 0), (w, 0), (0, 0)))[:, :s]
        count = jnp.minimum(jnp.arange(1, s + 1, dtype=jnp.float32), float(w))
        groups.append((cs_g - lagged) / count[None, :, None] - uf[..., sl])
    p = jnp.stack(groups, axis=2).astype(u.dtype)
    y = jnp.einsum('bsgc,gcd->bsgd', p, w_pool).reshape(b, s, POOL_WIDTH)
    return y * pool_scale


def short_conv(h, b_gate, c_gate, conv_w):
    u = c_gate * h
    y = lax.conv_general_dilated(
        u, conv_w[:, None, :].astype(u.dtype), window_strides=(1,), padding=[(CONV_K - 1, 0)],
        dimension_numbers=('NWC', 'WIO', 'NWC'), feature_group_count=CONV_WIDTH)
    return b_gate * y


def hybrid_layer(x, c, w_ada, b_ada, g_mix_pre, g_mix_post, g_ff_pre, g_ff_post, w_in, b_f,
                 w_pool, pool_scale, conv_w, w_branch, w_out, w_ff1, w_ff2):
    b, s, d = x.shape
    mod = jax.nn.silu(c) @ w_ada + b_ada
    shift_m, scale_m, gate_m, shift_f, scale_f, gate_f = jnp.split(mod, 6, axis=-1)

    h = rmsnorm(x, g_mix_pre) * (1.0 + scale_m[:, None]) + shift_m[:, None]
    z = h @ w_in
    sizes = [A_WIDTH, A_WIDTH, A_WIDTH, A_HEADS, POOL_WIDTH, CONV_WIDTH, CONV_WIDTH, CONV_WIDTH]
    cuts = [int(v) for v in np.cumsum(sizes)]
    q, k, v, fl, pu, ch, cb, cc, gl = jnp.split(z, cuts, axis=-1)
    heads = (b, s, A_HEADS, A_HEAD_DIM)
    br_a = forgetting_attention(q.reshape(heads), k.reshape(heads), v.reshape(heads), fl + b_f)
    br_b = pool_mixer(pu, w_pool, pool_scale)
    br_c = short_conv(ch, cb, cc, conv_w)
    gates = jax.nn.sigmoid(gl).reshape(b, s, N_BRANCH, d)
    wa = w_branch[:A_WIDTH]
    wb = w_branch[A_WIDTH:A_WIDTH + POOL_WIDTH]
    wc = w_branch[A_WIDTH + POOL_WIDTH:]
    merged = gates[:, :, 0] * (br_a @ wa) + gates[:, :, 1] * (br_b @ wb) + gates[:, :, 2] * (br_c @ wc)
    y = merged @ w_out
    x = x + gate_m[:, None] * rmsnorm(y, g_mix_post)

    h2 = rmsnorm(x, g_ff_pre) * (1.0 + scale_f[:, None]) + shift_f[:, None]
    y2 = jnp.square(jax.nn.relu(h2 @ w_ff1)) @ w_ff2
    return x + gate_f[:, None] * rmsnorm(y2, g_ff_post)


def setup_inputs(seed: int = 0) -> dict:
    key = jax.random.key(seed)
    ks = jax.random.split(key, 20)

    def nrm(k, shape, std):
        return jax.random.normal(k, shape, jnp.float32) * std

    d = D_MODEL
    branch_row_scale = jnp.concatenate([
        jnp.full((A_WIDTH,), A_WIDTH ** -0.5, jnp.float32),
        jnp.full((POOL_WIDTH,), POOL_WIDTH ** -0.5, jnp.float32),
        jnp.full((CONV_WIDTH,), CONV_WIDTH ** -0.5, jnp.float32)])
    return {
        'x': nrm(ks[0], (BATCH, SEQ, d), 1.0),
        'c': nrm(ks[1], (BATCH, d), 1.0),
        'w_ada': nrm(ks[2], (DEPTH, d, 6 * d), 0.5 * d ** -0.5),
        'b_ada': nrm(ks[3], (DEPTH, 6 * d), 0.02),
        'g_mix_pre': 1.0 + nrm(ks[4], (DEPTH, d), 0.05),
        'g_mix_post': 1.0 + nrm(ks[5], (DEPTH, d), 0.05),
        'g_ff_pre': 1.0 + nrm(ks[6], (DEPTH, d), 0.05),
        'g_ff_post': 1.0 + nrm(ks[7], (DEPTH, d), 0.05),
        'w_in': nrm(ks[8], (DEPTH, d, IN_COLS), d ** -0.5),
        'b_f': 3.0 + nrm(ks[9], (DEPTH, A_HEADS), 0.1),
        'w_pool': nrm(ks[10], (DEPTH, POOL_GROUPS, POOL_GROUP_DIM, POOL_GROUP_DIM), POOL_GROUP_DIM ** -0.5),
        'pool_scale': 1.0 + nrm(ks[11], (DEPTH, POOL_WIDTH), 0.1),
        'conv_w': nrm(ks[12], (DEPTH, CONV_K, CONV_WIDTH), CONV_K ** -0.5),
        'w_branch': nrm(ks[13], (DEPTH, MIX_WIDTH, d), 1.0) * branch_row_scale[None, :, None],
        'w_out': nrm(ks[14], (DEPTH, d, d), d ** -0.5),
        'w_ff1': nrm(ks[15], (DEPTH, d, D_FF), d ** -0.5),
        'w_ff2': nrm(ks[16], (DEPTH, D_FF, d), D_FF ** -0.5),
    }


def reference(x, c, w_ada, b_ada, g_mix_pre, g_mix_post, g_ff_pre, g_ff_post, w_in, b_f,
              w_pool, pool_scale, conv_w, w_branch, w_out, w_ff1, w_ff2):
    for l in range(DEPTH):
        x = hybrid_layer(x, c, w_ada[l], b_ada[l], g_mix_pre[l], g_mix_post[l], g_ff_pre[l], g_ff_post[l],
                         w_in[l], b_f[l], w_pool[l], pool_scale[l], conv_w[l], w_branch[l], w_out[l],
                         w_ff1[l], w_ff2[l])
    return x
```

```python
import numpy as np
from contextlib import ExitStack
import concourse.bass as bass
import concourse.mybir as mybir
from concourse.bass_utils import run_bass_kernel_spmd

F32 = mybir.dt.float32
BF16 = mybir.dt.bfloat16
AF = mybir.ActivationFunctionType
ALU = mybir.AluOpType

D = 1024
S = 2048
KD = 8
TT = 512
NT = 4
DEPTH = 2
DFF = 4096
IN_COLS = 5640
C_Q, C_K, C_V, C_F, C_PU, C_CH, C_CB, C_CC, C_G = 0, 512, 1024, 1536, 1544, 1800, 2056, 2312, 2568
NV = 216
V_BADA, V_GMPRE, V_GMPOST, V_GFPRE, V_GFPOST, V_PSCALE, V_CONVW, V_BF = 0, 48, 56, 64, 72, 80, 82, 88
NCON = 162
NCON_D = 418
CON_ID, CON_MNEG = 162, 290
MASK_NEG = -30000.0
CON_MASK, CON_INVW, CON_INVC = 0, 128, 130
RING = 3
RMS_EPS = 1e-6
ENGS = ("pe", "act", "dve", "pool", "sp")


class _Op:
    __slots__ = ("fn", "deps", "marked", "dma", "waits")

    def __init__(self, fn, dma):
        self.fn = fn
        self.dma = dma
        self.deps = []
        self.marked = False
        self.waits = []


class Prog:
    def __init__(self):
        self.ops = {e: [] for e in ENGS}
        self.last_w = {}
        self.readers = {}
        self.n_dma_sems = 0
        self.dma_cnt = {}

    def new_dma_sem(self):
        i = self.n_dma_sems
        self.n_dma_sems += 1
        self.dma_cnt[i] = 0
        return i

    def op(self, eng, fn, reads=(), writes=(), dma_sem=None):
        lst = self.ops[eng]
        seq = len(lst)
        dma = None
        if dma_sem is not None:
            self.dma_cnt[dma_sem] += 16
            dma = (dma_sem, self.dma_cnt[dma_sem])
            tok = ("d", dma_sem, dma[1])
        else:
            tok = ("c", eng, seq)
        o = _Op(fn, dma)
        raw = set()
        wdeps = set()
        for k in reads:
            t = self.last_w.get(k)
            if t is not None:
                raw.add(t)
        for k in writes:
            t = self.last_w.get(k)
            if t is not None:
                wdeps.add(t)
            for t in self.readers.get(k, ()):
                wdeps.add(t)
        for t in raw:
            if t[0] == "c" and t[1] == eng and dma is None and eng == "pe":
                continue
            o.deps.append(t)
        for t in wdeps - raw:
            if t[0] == "c" and t[1] == eng and dma is None:
                continue
            o.deps.append(t)
        for k in reads:
            self.readers.setdefault(k, []).append(tok)
        for k in writes:
            self.last_w[k] = tok
            self.readers[k] = []
        lst.append(o)
        return tok

    def handoff(self, dead_keys, new_keys):
        best = {}
        for k in dead_keys:
            toks = list(self.readers.get(k, ()))
            t = self.last_w.get(k)
            if t is not None:
                toks.append(t)
            for t in toks:
                key = (t[0], t[1])
                if key not in best or best[key][2] < t[2]:
                    best[key] = t
        toks = list(best.values())
        for nk in new_keys:
            self.readers.setdefault(nk, []).extend(toks)

    def finalize(self, nc):
        for e in ENGS:
            obs = {}
            for o in self.ops[e]:
                best = {}
                for t in o.deps:
                    key = (t[0], t[1])
                    if obs.get(key, -1) >= t[2]:
                        continue
                    if key not in best or best[key][2] < t[2]:
                        best[key] = t
                o.waits = list(best.values())
                for t in o.waits:
                    obs[(t[0], t[1])] = t[2]
                    if t[0] == "c":
                        self.ops[t[1]][t[2]].marked = True
        cum = {}
        for e in ENGS:
            c = 0
            arr = []
            for o in self.ops[e]:
                if o.marked:
                    c += 1
                arr.append(c)
            cum[e] = arr
        self.stats = {e: (len(self.ops[e]), cum[e][-1] if cum[e] else 0,
                          sum(len(o.waits) for o in self.ops[e])) for e in ENGS}
        with ExitStack() as st:
            esem = {e: st.enter_context(nc.semaphore("s_" + e)) for e in ENGS}
            dsem = [st.enter_context(nc.semaphore("d_%d" % i)) for i in range(self.n_dma_sems)]
            block = st.enter_context(nc.Block())

            def run(e, eng):
                for o in self.ops[e]:
                    for t in o.waits:
                        if t[0] == "c":
                            eng.wait_ge(esem[t[1]], cum[t[1]][t[2]])
                        else:
                            eng.wait_ge(dsem[t[1]], t[2])
                    ins = o.fn(eng)
                    if o.dma is not None:
                        ins.then_inc(dsem[o.dma[0]], 16)
                    elif o.marked:
                        ins.then_inc(esem[e], 1)

            @block.tensor
            def _(eng):
                run("pe", eng)

            @block.scalar
            def _(eng):
                run("act", eng)

            @block.vector
            def _(eng):
                run("dve", eng)

            @block.gpsimd
            def _(eng):
                run("pool", eng)

            @block.sync
            def _(eng):
                run("sp", eng)
                for i in range(self.n_dma_sems):
                    if self.dma_cnt[i] > 0:
                        eng.wait_ge(dsem[i], self.dma_cnt[i])


def build_nc(dbg=False, n_layers=DEPTH, stop_after_mix=False):
    nc = bass.Bass("TRN2", target_bir_lowering=False)
    DBGN = 16 * 16384
    dbg_d = nc.dram_tensor("dbg", [128, DBGN], F32, kind="ExternalOutput").ap() if dbg else None
    xT_d = nc.dram_tensor("xT", [D, S], F32, kind="ExternalInput").ap()
    cT_d = nc.dram_tensor("cT", [128, KD], F32, kind="ExternalInput").ap()
    vecs_d = nc.dram_tensor("vecs", [DEPTH, 128, NV], F32, kind="ExternalInput").ap()
    cons_d = nc.dram_tensor("cons", [128, NCON_D], F32, kind="ExternalInput").ap()
    w_ada_d = nc.dram_tensor("w_ada", [DEPTH, D, 6 * D], F32, kind="ExternalInput").ap()
    w_in_d = nc.dram_tensor("w_in", [DEPTH, D, IN_COLS], F32, kind="ExternalInput").ap()
    w_pool_d = nc.dram_tensor("w_pool", [DEPTH, 4, 64, 64], F32, kind="ExternalInput").ap()
    w_br_d = nc.dram_tensor("w_branch", [DEPTH, D, D], F32, kind="ExternalInput").ap()
    w_out_d = nc.dram_tensor("w_out", [DEPTH, D, D], F32, kind="ExternalInput").ap()
    w_ff1_d = nc.dram_tensor("w_ff1", [DEPTH, D, DFF], F32, kind="ExternalInput").ap()
    w_ff2_d = nc.dram_tensor("w_ff2", [DEPTH, DFF, D], F32, kind="ExternalInput").ap()
    out_d = nc.dram_tensor("outT", [D, S], F32, kind="ExternalOutput").ap()

    st = ExitStack()
    with st:
        def sb(name, shape, dt):
            return st.enter_context(nc.sbuf_tensor(name, shape, dt))

        x_sb = sb("x_sb", [128, KD, S], F32)
        ring = sb("ring", [128, RING, 4096], BF16)
        rstd = sb("rstd", [128, S], F32)
        Wt = sb("W", [128, 49152], BF16)
        vec_sb = sb("vec_sb", [128, DEPTH, NV], F32)
        cons_sb = sb("cons_sb", [128, NCON], F32)
        ident_bf = sb("ident_bf", [128, 128], BF16)
        mneg_bf = sb("mneg_bf", [128, 128], BF16)
        ones_bf = sb("ones_bf", [128, 128], BF16)
        ones_f = sb("ones_f", [128, 128], F32)
        mod_sb = sb("mod_sb", [128, DEPTH, 48], F32)
        der = sb("der", [128, DEPTH, 32], F32)
        c_sb = sb("c_sb", [128, KD], F32)
        sc_bf = sb("sc_bf", [128, KD], BF16)
        wf_bf = sb("wf_bf", [128, KD, 8], BF16)
        wpool_bf = sb("wpool_bf", [128, 4, 128], BF16)
        sq = sb("sq", [128, 2, TT], BF16)
        tmp = sb("tmp", [128, 2, TT], F32)
        eps_t = sb("eps_t", [128, 1], F32)
        sp_t = sb("sp_t", [128, 128], F32)
        fsc = sb("fsc", [128, 4, 128], F32)
        Gh = sb("Gh", [128, 8, 16], F32)
        ps = [st.enter_context(nc.psum_tensor("ps%d" % i, [128, TT], F32)) for i in range(8)]

        P = Prog()
        s_misc = [P.new_dma_sem() for _ in range(4)]
        s_x = [P.new_dma_sem() for _ in range(NT)]
        s_out = [P.new_dma_sem() for _ in range(KD)]
        s_ring = [P.new_dma_sem() for _ in range(RING)]
        s_mask = P.new_dma_sem()
        s_mask2 = P.new_dma_sem()
        s_wp = [P.new_dma_sem() for _ in range(8)]
        s_wf = P.new_dma_sem()

        def MM(out, lhsT, rhs, start, stop, reads, writes):
            P.op("pe", lambda e: e.matmul(out, lhsT=lhsT, rhs=rhs, start=start, stop=stop), reads, writes)

        def ACT(out, in_, func, reads, writes, bias=None, scale=None):
            kw = {}
            if bias is not None:
                kw["bias"] = bias
            if scale is not None:
                kw["scale"] = scale
            P.op("act", lambda e: e.activation(out=out, in_=in_, func=func, **kw), reads, writes)

        def TT_(out, in0, in1, op, reads, writes):
            P.op("dve", lambda e: e.tensor_tensor(out=out, in0=in0, in1=in1, op=op), reads, writes)

        def TS(out, in0, s1, s2, op0, op1, reads, writes):
            if op1 is None:
                P.op("dve", lambda e: e.tensor_scalar(out=out, in0=in0, scalar1=s1, scalar2=None, op0=op0),
                     reads, writes)
            else:
                P.op("dve", lambda e: e.tensor_scalar(out=out, in0=in0, scalar1=s1, scalar2=s2, op0=op0, op1=op1),
                     reads, writes)

        def STT(out, in0, scalar, in1, op0, op1, reads, writes):
            P.op("dve", lambda e: e.scalar_tensor_tensor(out=out, in0=in0, scalar=scalar, in1=in1, op0=op0, op1=op1),
                 reads, writes)

        def VCOPY(out, in_, reads, writes):
            P.op("dve", lambda e: e.tensor_copy(out=out, in_=in_), reads, writes)

        def MEMSET(ap, val, writes):
            P.op("dve", lambda e: e.memset(ap, val), (), writes)

        def RECIP(out, in_, reads, writes):
            P.op("dve", lambda e: e.reciprocal(out=out, in_=in_), reads, writes)

        def DMA(q, out, in_, reads, writes, sem):
            P.op(q, lambda e: e.dma_start(out=out, in_=in_), reads, writes, dma_sem=sem)

        dbg_off = {}
        dbg_sems = []

        def dump(name, view2d, n, reads):
            if not dbg:
                return
            o = len(dbg_off) * 16384
            dbg_off[name] = (o, n)
            sem = P.new_dma_sem()
            DMA("pool", dbg_d[:, o:o + n], view2d, reads, (), sem)

        evac_ctr = [0]

        def EVAC(out, in_, reads, writes):
            evac_ctr[0] += 1
            if evac_ctr[0] % 2 == 0:
                ACT(out, in_, AF.Copy, reads, writes)
            else:
                VCOPY(out, in_, reads, writes)

        arena = []

        def alloc(lo, hi, keys):
            dead = []
            keep = []
            for ent in arena:
                if ent[0] < hi and lo < ent[1]:
                    dead.extend(ent[2])
                else:
                    keep.append(ent)
            arena[:] = keep
            if dead:
                P.handoff(dead, keys)
            arena.append((lo, hi, list(keys)))

        def wbf(lo_bytes, nbytes):
            return Wt[:, lo_bytes // 2:(lo_bytes + nbytes) // 2]

        def wf32(lo_bytes, nbytes):
            return Wt[:, lo_bytes // 2:(lo_bytes + nbytes) // 2].bitcast(F32)

        K1 = 1024

        ring_state = {"n": 0}

        def ring_keys(s):
            return [("ring", s, i) for i in range(4)]

        def ring_load(pieces):
            s = ring_state["n"] % RING
            ring_state["n"] += 1
            n = len(pieces)
            for i, (dst_fn, src) in enumerate(pieces):
                wk = [("ring", s, i)] if i < n - 1 else [("ring", s, j) for j in range(i, 4)]
                DMA("pool", dst_fn(ring[:, s, :]), src, (), wk, s_ring[s])
            final_tok = ("d", s_ring[s], P.dma_cnt[s_ring[s]])
            for kk in ring_keys(s):
                P.last_w[kk] = final_tok
            return s

        items = []

        def run_items():
            widx = [i for i, it in enumerate(items) if it[0] is not None]
            slot_of = {}
            loaded = 0
            done_w = 0
            for i, (pieces, fn) in enumerate(items):
                while loaded < len(widx) and loaded < done_w + RING:
                    j = widx[loaded]
                    slot_of[j] = ring_load(items[j][0])
                    loaded += 1
                fn(slot_of.get(i))
                if pieces is not None:
                    done_w += 1

        def xk(k, t):
            return ("x", k, t)

        def tsl(t):
            return slice(t * TT, (t + 1) * TT)

        def setup(_):
            DMA("sp", cons_sb[:], cons_d[:, 0:NCON], (), ["cons"], s_misc[0])
            DMA("sp", c_sb[:], cT_d, (), ["c"], s_misc[1])
            for l in range(DEPTH):
                DMA("sp", vec_sb[:, l, :], vecs_d[l], (), [("vec", l)], s_misc[2 + l])
            xv = xT_d.rearrange("(k p) t -> p k t", p=128)
            for t in range(NT):
                for k in range(KD):
                    DMA("sp", x_sb[:, k, tsl(t)], xv[:, k, tsl(t)], (), [xk(k, t)], s_x[t])
                ftok = ("d", s_x[t], P.dma_cnt[s_x[t]])
                for k in range(KD):
                    P.last_w[xk(k, t)] = ftok
            DMA("pool", ident_bf[:], cons_d[:, CON_ID:CON_ID + 128], (), ["ident"], s_mask)
            DMA("pool", mneg_bf[:], cons_d[:, CON_MNEG:CON_MNEG + 128], (), ["mneg"], s_mask2)
            MEMSET(ones_bf[:], 1.0, ["ones_bf"])
            MEMSET(ones_f[:], 1.0, ["ones_f"])
            MEMSET(eps_t[:], RMS_EPS, ["eps"])
            MEMSET(wpool_bf[:], 0.0, [("wpool", l, g) for l in range(DEPTH) for g in range(4)])
            for l in range(DEPTH):
                for g in range(4):
                    r0 = (g % 2) * 64
                    DMA("pool", wpool_bf[r0:r0 + 64, l * 2 + g // 2, r0:r0 + 64], w_pool_d[l, g], (),
                        [("wpool", l, g)], s_wp[l * 4 + g])
            ACT(sc_bf[:], c_sb[:], AF.Silu, ["c"], ["sc"])

        items.append((None, setup))

        def mod_item_list(l):
            wv = w_ada_d[l].rearrange("(k p) n -> p k n", p=128)
            out = []

            def mk(g):
                def fn(s):
                    slot = ring[:, s, :].rearrange("p (k n) -> p k n", k=KD)
                    mbank = 2 + g % 6
                    pm = ps[mbank]
                    ml = mod_sb[:, l, :]
                    dl = der[:, l, :]
                    vl = vec_sb[:, l, :]
                    mk_, dk_ = ("mod", l), ("der", l)
                    for jj in range(4):
                        for k in range(KD):
                            MM(pm[:, jj:jj + 1], slot[:, k, jj * 128:(jj + 1) * 128], sc_bf[:, k:k + 1],
                               k == 0, k == KD - 1, ring_keys(s) + ["sc"], [("ps", mbank)])
                    TT_(ml[:, g * 4:(g + 1) * 4], pm[:, 0:4], vl[:, V_BADA + g * 4:V_BADA + (g + 1) * 4], ALU.add,
                        [("ps", mbank), ("vec", l)], [mk_])
                    if g == 3:
                        STT(dl[:, 0:8], ml[:, 8:16], 1.0, vl[:, V_GMPRE:V_GMPRE + 8], ALU.add, ALU.mult,
                            [mk_, ("vec", l)], [dk_])
                    if g == 5:
                        TT_(dl[:, 8:16], ml[:, 16:24], vl[:, V_GMPOST:V_GMPOST + 8], ALU.mult, [mk_, ("vec", l)], [dk_])
                    if g == 9:
                        STT(dl[:, 16:24], ml[:, 32:40], 1.0, vl[:, V_GFPRE:V_GFPRE + 8], ALU.add, ALU.mult,
                            [mk_, ("vec", l)], [dk_])
                    if g == 11:
                        TT_(dl[:, 24:32], ml[:, 40:48], vl[:, V_GFPOST:V_GFPOST + 8], ALU.mult, [mk_, ("vec", l)], [dk_])
                pieces = [(lambda sl: sl.rearrange("p (k n) -> p k n", k=KD), wv[:, :, g * 512:(g + 1) * 512])]
                return (pieces, fn)

            for g in range(12):
                out.append(mk(g))
            return out

        def norm_stats(t, src_fn, src_keys_fn, split=False):
            bank = t % 2
            for k in range(KD):
                if split and k % 2 == 1:
                    TT_(sq[:, k % 2, :], src_fn(k, t), src_fn(k, t), ALU.mult, [src_keys_fn(k, t)], [("sq", k % 2)])
                else:
                    ACT(sq[:, k % 2, :], src_fn(k, t), AF.Square, [src_keys_fn(k, t)], [("sq", k % 2)])
                MM(ps[bank][:], ones_bf[:], sq[:, k % 2, :], k == 0, k == KD - 1,
                   ["ones_bf", ("sq", k % 2)], [("ps", bank)])
            ACT(tmp[:, 0, :], ps[bank][:], AF.Ln, [("ps", bank), "eps"], [("tmp", 0)],
                bias=eps_t[:], scale=1.0 / D)
            ACT(rstd[:, tsl(t)], tmp[:, 0, :], AF.Exp, [("tmp", 0)], [("rstd", t)], scale=-0.5)

        def modulate(l, t, dst_fn, dst_key_fn, col0):
            sh0 = 0 if col0 == 0 else 24
            for k in range(KD):
                TT_(tmp[:, k % 2, :], x_sb[:, k, tsl(t)], rstd[:, tsl(t)], ALU.mult,
                    [xk(k, t), ("rstd", t)], [("tmp", k % 2)])
                if False:
                    TS(dst_fn(k), tmp[:, k % 2, :], der[:, l, col0 + k:col0 + k + 1], mod_sb[:, l, sh0 + k:sh0 + k + 1],
                       ALU.mult, ALU.add, [("tmp", k % 2), ("der", l), ("mod", l)], [dst_key_fn(k)])
                else:
                    ACT(dst_fn(k), tmp[:, k % 2, :], AF.Identity, [("tmp", k % 2), ("der", l), ("mod", l)],
                        [dst_key_fn(k)], bias=mod_sb[:, l, sh0 + k:sh0 + k + 1],
                        scale=der[:, l, col0 + k:col0 + k + 1])

        def residual_update(l, t, y_fn, y_key_fn, gcol0):
            for k in range(KD):
                TT_(tmp[:, k % 2, :], y_fn(k), rstd[:, tsl(t)], ALU.mult, [y_key_fn(k), ("rstd", t)],
                    [("tmp", k % 2)])
                STT(x_sb[:, k, tsl(t)], tmp[:, k % 2, :], der[:, l, gcol0 + k:gcol0 + k + 1], x_sb[:, k, tsl(t)],
                    ALU.mult, ALU.add, [("tmp", k % 2), ("der", l), xk(k, t)], [xk(k, t)])

        def layer_items(l):
            win = w_in_d[l].rearrange("(k p) n -> p k n", p=128)
            wbr = w_br_d[l].rearrange("(k p) n -> p k n", p=128)
            wout = w_out_d[l].rearrange("(k p) n -> p k n", p=128)
            wff1 = w_ff1_d[l].rearrange("(k p) n -> p k n", p=128)
            wff2 = w_ff2_d[l].rearrange("(k p) n -> p k n", p=128)
            tag = "L%d" % l

            def k8(sl):
                return sl.rearrange("p (k n) -> p k n", k=KD)

            hT = wbf(0, 32 * K1).rearrange("p (k t) -> p k t", k=KD)
            qT = wbf(32 * K1, 16 * K1).rearrange("p (c t) -> p c t", c=4)
            kT = wbf(48 * K1, 16 * K1).rearrange("p (c t) -> p c t", c=4)
            V_sb = wbf(64 * K1, 24 * K1).rearrange("p (a b c) -> p a b c", a=16, b=4)
            bias_sb = wf32(88 * K1, 2 * K1).rearrange("p (h g j) -> p h g j", h=8, g=4)
            brA = wbf(0, 16 * K1).rearrange("p (c t) -> p c t", c=4)
            PT = wbf(16 * K1, 8 * K1).rearrange("p (n t) -> p n t", n=8)
            hT2 = wbf(32 * K1, 32 * K1).rearrange("p (k t) -> p k t", k=KD)
            temps = wf32(64 * K1, 32 * K1).rearrange("p (n t) -> p n t", n=4)
            pbf = wbf(64 * K1 + 8 * K1, 4 * K1)
            brBC = wbf(16 * K1, 16 * K1).rearrange("p (c t) -> p c t", c=4)
            merged = wbf(64 * K1, 32 * K1).rearrange("p (k t) -> p k t", k=KD)
            y_sb = wf32(0, 64 * K1).rearrange("p (k t) -> p k t", k=KD)

            def hk(k, t):
                return ("hT", tag, k, t)

            def h2k(k, t):
                return ("hT2", tag, k, t)

            def p1(_):
                alloc(0, 32 * K1, [hk(k, t) for k in range(KD) for t in range(NT)])
                if l == 0:
                    for t in range(NT):
                        norm_stats(t, lambda k, t_: x_sb[:, k, tsl(t_)], xk, split=False)
                    for t in range(NT):
                        modulate(l, t, lambda k, t_=t: hT[:, k, tsl(t_)], lambda k, t_=t: hk(k, t_), 0)
                else:
                    for t in range(NT):
                        norm_stats(t, lambda k, t_: x_sb[:, k, tsl(t_)], xk, split=False)
                        modulate(l, t, lambda k, t_=t: hT[:, k, tsl(t_)], lambda k, t_=t: hk(k, t_), 0)
                DMA("pool", wf_bf[:], win[:, :, C_F:C_F + 8], (), ["wf"], s_wf)

            items.append((None, p1))
            if l == 0:
                items.append((None, lambda _: dump("hT", wbf(0, 32 * K1), 16384, [hk(k, t) for k in range(KD) for t in range(NT)])))

            def vkey(tt):
                return ("V", tag, tt)

            def p3v(s):
                slot = k8(ring[:, s, :])
                alloc(64 * K1, 88 * K1, [vkey(tt) for tt in range(16)])
                alloc(88 * K1, 96 * K1, [("bias", tag)])
                MEMSET(V_sb[:, :, :, 64:128], 1.0, [vkey(tt) for tt in range(16)])
                for tt in range(16):
                    bank = 2 + tt % 2
                    t = tt // 4
                    for k in range(KD):
                        lhs = hT[:, k, tt * 128:(tt + 1) * 128]
                        MM(ps[bank][:], lhs, slot[:, k, :], k == 0, k == KD - 1,
                           [hk(k, t)] + ring_keys(s), [("ps", bank)])
                        MM(ps[6][:, tt * 8:(tt + 1) * 8], lhs, wf_bf[:, k, :], k == 0, k == KD - 1,
                           [hk(k, t), "wf"], [("ps", 6)])
                    pv = ps[bank][:].rearrange("p (a b) -> p a b", b=128)
                    ACT(V_sb[:, tt, :, 0:64], pv[:, :, 0:64], AF.Copy, [("ps", bank)], [vkey(tt)])
                    VCOPY(V_sb[:, tt, :, 128:192], pv[:, :, 64:128], [("ps", bank)], [vkey(tt)])
                TT_(sp_t[:], ps[6][:, 0:128], vec_sb[:, l, V_BF:V_BF + 128], ALU.add, [("ps", 6), ("vec", l)], ["sp"])
                ACT(sp_t[:], sp_t[:], AF.Exp, ["sp"], ["sp"], scale=-1.0)
                ACT(sp_t[:], sp_t[:], AF.Ln, ["sp"], ["sp"], bias=1.0)
                MM(ps[7][:, 0:128], cons_sb[:, CON_MASK:CON_MASK + 128], sp_t[:], True, True,
                   ["cons", "sp"], [("ps", 7)])
                MM(ps[7][:, 128:256], ones_f[:], sp_t[:], True, True, ["ones_f", "sp"], [("ps", 7)])
                VCOPY(fsc[:, 0, :], ps[7][:, 128:256], [("ps", 7)], ["fsc"])
                MEMSET(fsc[:, 1, 0:8], 0.0, ["fsc"])
                for i in range(1, 16):
                    TT_(fsc[:, 1, i * 8:(i + 1) * 8], fsc[:, 1, (i - 1) * 8:i * 8], fsc[:, 0, (i - 1) * 8:i * 8],
                        ALU.add, ["fsc"], ["fsc"])
                TT_(fsc[:, 2, :], fsc[:, 1, :], fsc[:, 0, :], ALU.add, ["fsc"], ["fsc"])
                TT_(Gh[:].rearrange("p h j -> p j h"), ps[7][:, 0:128].rearrange("p (j h) -> p j h", h=8),
                    fsc[:, 1, :].rearrange("p (j h) -> p j h", h=8), ALU.add, [("ps", 7), "fsc"], ["Gh"])
                for g in range(NT):
                    TT_(fsc[:, 3, g * 8:(g + 1) * 8], fsc[:, 1, (4 * g) * 8:(4 * g) * 8 + 8],
                        fsc[:, 2, (4 * g + 3) * 8:(4 * g + 3) * 8 + 8], ALU.add, ["fsc"], ["fsc"])
                TS(fsc[:, 3, 0:32], fsc[:, 3, 0:32], 0.5, None, ALU.mult, None, ["fsc"], ["fsc"])
                for h in range(8):
                    for g in range(NT):
                        TS(bias_sb[:, h, g, :], Gh[:, h, :], fsc[:, 3, g * 8 + h:g * 8 + h + 1], None,
                           ALU.subtract, None, ["Gh", "fsc"], [("bias", tag)])

            items.append(([(k8, win[:, :, C_V:C_V + 512])], p3v))
            if l == 0:
                for it in mod0[4:7]:
                    items.append(it)

            def proj_fm(slot, s, ncols_chunks, col_of_chunk, rhs, rhs_key, dst_fn, dst_key_fn, bankset_ctr):
                for c in range(ncols_chunks):
                    bs = (bankset_ctr[0] % 2) * 4
                    bankset_ctr[0] += 1
                    for k in range(KD):
                        for t in range(NT):
                            MM(ps[bs + t][:], slot[:, k, col_of_chunk(c):col_of_chunk(c) + 128], rhs[:, k, tsl(t)],
                               k == 0, k == KD - 1, ring_keys(s) + [rhs_key(k, t)], [("ps", bs + t)])
                    for t in range(NT):
                        dst_fn(c, t, ps[bs + t][:], ("ps", bs + t))

            bsc = [0]

            def qkey(c, t):
                return ("q", tag, c, t)

            def kkey(c, t):
                return ("k", tag, c, t)

            def p3q(s):
                alloc(32 * K1, 48 * K1, [qkey(c, t) for c in range(4) for t in range(NT)])
                proj_fm(k8(ring[:, s, :]), s, 4, lambda c: c * 128, hT, hk,
                        lambda c, t, p_, pk: EVAC(qT[:, c, tsl(t)], p_, [pk], [qkey(c, t)]), None, bsc)

            def p3k(s):
                alloc(48 * K1, 64 * K1, [kkey(c, t) for c in range(4) for t in range(NT)])
                proj_fm(k8(ring[:, s, :]), s, 4, lambda c: c * 128, hT, hk,
                        lambda c, t, p_, pk: EVAC(kT[:, c, tsl(t)], p_, [pk], [kkey(c, t)]), None, bsc)

            items.append(([(k8, win[:, :, C_Q:C_Q + 512])], p3q))
            if l == 0:
                for it in mod0[7:10]:
                    items.append(it)
            items.append(([(k8, win[:, :, C_K:C_K + 512])], p3k))
            if l == 0:
                for it in mod0[10:12]:
                    items.append(it)
            if l == 0:
                def dmp1(_):
                    dump("Gh", Gh[:].rearrange("p h j -> p (h j)"), 128, ["Gh"])
                    dump("fsc", fsc[:].rearrange("p a b -> p (a b)"), 512, ["fsc"])
                    dump("bias", wf32(88 * K1, 2 * K1), 512, [("bias", tag)])
                    dump("V", wbf(64 * K1, 24 * K1), 12288, [vkey(tt) for tt in range(16)])
                    dump("q", wbf(32 * K1, 16 * K1), 8192, [qkey(c, t) for c in range(4) for t in range(NT)])
                    dump("k", wbf(48 * K1, 16 * K1), 8192, [kkey(c, t) for c in range(4) for t in range(NT)])
                items.append((None, dmp1))

            def brk(c, t):
                return ("br", tag, c, t)

            steps = []
            for h in range(8):
                for g in range(NT):
                    for j in range(4 * g + 4):
                        steps.append((h, g, j))
            BATCH = 3
            NFILL = 3
            FILL_N = 480
            batches = [list(range(i, min(i + BATCH, len(steps)))) for i in range(0, len(steps), BATCH)]
            att_state = {"m": 0}

            def sbank(n):
                return (n // BATCH % 2) * BATCH + n % BATCH

            def emit_S_batch(m):
                ns = batches[m]
                set_banks = [("ps", (m % 2) * BATCH + i) for i in range(BATCH)]
                first = True
                for _i in range(NFILL):
                    MM(ps[(m % 2) * BATCH][:, 0:FILL_N], ones_bf[:], qT[:, 0, 0:FILL_N], True, True, ["ones_bf", qkey(0, 0)],
                       set_banks if first else [set_banks[0]])
                    first = False
                for n in ns:
                    h, g, j = steps[n]
                    p, hh = h // 2, h % 2
                    pr = slice(hh * 64, hh * 64 + 64)
                    q0 = max(g * TT, j * 128)
                    off = q0 - g * TT
                    b = sbank(n)
                    diag = j >= 4 * g
                    MM(ps[b][:, off:TT], kT[pr, p, j * 128:(j + 1) * 128], qT[pr, p, q0:(g + 1) * TT], True, not diag,
                       [kkey(p, j // 4), qkey(p, g)], set_banks if first else [("ps", b)])
                    first = False
                    if diag:
                        MM(ps[b][:, off:off + 128], ident_bf[:], mneg_bf[:], False, True, ["ident", "mneg"], [("ps", b)])
                for n in ns:
                    h, g, j = steps[n]
                    q0 = max(g * TT, j * 128)
                    off = q0 - g * TT
                    b = sbank(n)
                    pt = n % (2 * BATCH)
                    ACT(PT[:, pt, off:TT], ps[b][:, off:TT], AF.Exp, [("ps", b), ("bias", tag)],
                        [("PT", tag, pt)], bias=bias_sb[:, h, g, j:j + 1], scale=0.125)

            def emit_PV_batch(m):
                ns = batches[m]
                all_pt = [("PT", tag, n % (2 * BATCH)) for n in ns]
                first = True
                for n in ns:
                    h, g, j = steps[n]
                    p, hh = h // 2, h % 2
                    pr = slice(hh * 64, hh * 64 + 64)
                    lr = slice((1 - hh) * 64, (1 - hh) * 64 + 64)
                    q0 = max(g * TT, j * 128)
                    off = q0 - g * TT
                    pt = n % (2 * BATCH)
                    gi = h * NT + g
                    ob = 6 + gi % 2
                    MM(ps[ob][:, off:TT], V_sb[:, j, p, hh * 64:hh * 64 + 128], PT[:, pt, off:TT],
                       j == 0, j == 4 * g + 3, [vkey(j)] + (all_pt if first else [("PT", tag, pt)]), [("ps", ob)])
                    first = False
                    if j == 4 * g + 3:
                        ti = gi % 2
                        RECIP(tmp[lr, ti, :], ps[ob][lr, :], [("ps", ob)], [("tmp", ti)])
                        TT_(brA[pr, p, tsl(g)], ps[ob][pr, :], tmp[lr, ti, :], ALU.mult,
                            [("ps", ob), ("tmp", ti)], [brk(p, g)])

            def p3att(h):
                def fn(_):
                    if h == 0:
                        alloc(0, 16 * K1, [brk(c, t) for c in range(4) for t in range(NT)])
                        alloc(16 * K1, 24 * K1, [("PT", tag, i) for i in range(8)])
                    nb = len(batches)
                    hi_m = ((h + 1) * nb) // 8 + (1 if h == 7 else 0)
                    while att_state["m"] < hi_m:
                        m = att_state["m"]
                        if m < nb:
                            emit_S_batch(m)
                        if m >= 1:
                            emit_PV_batch(m - 1)
                        att_state["m"] += 1
                return fn

            modl1 = mod_item_list(1) if (l == 0 and n_layers > 1) else []
            mod_in_p5b = bool(modl1) and not (stop_after_mix or dbg)
            for h in range(8):
                items.append((None, p3att(h)))
                if not mod_in_p5b:
                    for it in modl1[2 * h:2 * h + 2]:
                        items.append(it)
            if l == 0:
                items.append((None, lambda _: dump("brA", wbf(0, 16 * K1), 8192, [brk(c, t) for c in range(4) for t in range(NT)])))

            def p1b(_):
                alloc(32 * K1, 64 * K1, [h2k(k, t) for k in range(KD) for t in range(NT)])
                for t in range(NT):
                    modulate(l, t, lambda k, t_=t: hT2[:, k, tsl(t_)], lambda k, t_=t: h2k(k, t_), 0)

            items.append((None, p1b))

            def p2(c):
                def fn(s):
                    slot = k8(ring[:, s, :])
                    if c == 0:
                        alloc(64 * K1, 96 * K1, [("tp", tag, n, t) for n in range(4) for t in range(NT)])
                        alloc(16 * K1, 32 * K1, [brk(4 + cc, t) for cc in range(4) for t in range(NT)])
                    proj_fm(slot, s, 4, lambda n: n * 128, hT2, h2k,
                            lambda n, t, p_, pk: EVAC(temps[:, n, tsl(t)], p_, [pk], [("tp", tag, n, t)]), None, bsc)
                    ch, cb, cc_, pu = temps[:, 0, :], temps[:, 1, :], temps[:, 2, :], temps[:, 3, :]

                    def tk(n):
                        return [("tp", tag, n, t) for t in range(NT)]
                    cw = lambda tap: vec_sb[:, l, V_CONVW + tap * 2 + c:V_CONVW + tap * 2 + c + 1]
                    TT_(cc_, cc_, ch, ALU.mult, tk(2) + tk(0), tk(2))
                    TS(ch, cc_, cw(2), None, ALU.mult, None, tk(2) + [("vec", l)], tk(0))
                    STT(ch[:, 2:S], cc_[:, 0:S - 2], cw(0), ch[:, 2:S], ALU.mult, ALU.add, tk(2) + tk(0) + [("vec", l)], tk(0))
                    STT(ch[:, 1:S], cc_[:, 0:S - 1], cw(1), ch[:, 1:S], ALU.mult, ALU.add, tk(2) + tk(0) + [("vec", l)], tk(0))
                    TT_(brBC[:, 2 + c, :], ch, cb, ALU.mult, tk(0) + tk(1), [brk(6 + c, t) for t in range(NT)])
                    nsteps_lo, nsteps_hi = (1, 2) if c == 0 else (3, 4)
                    src = pu
                    bufs = [ch, cc_]
                    srck = tk(3)
                    bk = [tk(0), tk(2)]
                    bi = 0
                    lo = slice(0, 64)
                    hi = slice(64, 128)
                    allp = slice(0, 128)
                    for step in range(nsteps_hi):
                        sh = 1 << step
                        prs = allp if step < nsteps_lo else hi
                        dst, dk = bufs[bi], bk[bi]
                        TT_(dst[prs, sh:S], src[prs, sh:S], src[prs, 0:S - sh], ALU.add, srck, dk)
                        VCOPY(dst[prs, 0:sh], src[prs, 0:sh], srck, dk)
                        if step == nsteps_lo - 1 and nsteps_lo < nsteps_hi:
                            lo_src, lo_k = dst, dk
                        src, srck = dst, dk
                        bi = 1 - bi
                    hi_src, hi_k = src, srck
                    if nsteps_lo == nsteps_hi:
                        lo_src, lo_k = src, srck
                    invw = cons_sb[:, CON_INVW + c:CON_INVW + c + 1]
                    invc = cons_sb[:, CON_INVC + c * 16:CON_INVC + (c + 1) * 16]
                    pk_ = [("tp", tag, 1, t) for t in range(NT)]
                    for prs, ssrc, sk in ((lo, lo_src, lo_k), (hi, hi_src, hi_k)):
                        STT(pbf[prs, :], ssrc[prs, :], invw[prs, :], pu[prs, :], ALU.mult, ALU.subtract,
                            sk + tk(3) + ["cons"], pk_)
                        TT_(tmp[prs, 0, 0:16], ssrc[prs, 0:16], invc[prs, :], ALU.mult, sk + ["cons"], [("tmp", 0)])
                        TT_(pbf[prs, 0:16], tmp[prs, 0, 0:16], pu[prs, 0:16], ALU.subtract, [("tmp", 0)] + tk(3), pk_)
                    for t in range(NT):
                        bank = 6 + t % 2
                        MM(ps[bank][:], wpool_bf[:, l * 2 + c, :], pbf[:, tsl(t)], True, True,
                           [("wpool", l, 2 * c), ("wpool", l, 2 * c + 1)] + pk_, [("ps", bank)])
                        ACT(brBC[:, c, tsl(t)], ps[bank][:], AF.Identity, [("ps", bank), ("vec", l)], [brk(4 + c, t)],
                            scale=vec_sb[:, l, V_PSCALE + c:V_PSCALE + c + 1])
                pieces = []
                for n, col in enumerate((C_CH, C_CB, C_CC, C_PU)):
                    pieces.append((lambda sl, n=n: k8(sl)[:, :, n * 128:(n + 1) * 128],
                                   win[:, :, col + c * 128:col + (c + 1) * 128]))
                return (pieces, fn)

            items.append(p2(0))
            items.append(p2(1))
            if l == 0:
                items.append((None, lambda _: dump("pbf", wbf(72 * K1, 4 * K1), 2048, [("tp", tag, 1, t) for t in range(NT)])))
                items.append((None, lambda _: dump("pu", wf32(88 * K1, 8 * K1), 2048, [("tp", tag, 3, t) for t in range(NT)])))
                items.append((None, lambda _: dump("brBC", wbf(16 * K1, 16 * K1), 8192, [brk(4 + c, t) for c in range(4) for t in range(NT)])))

            def mk_(k, t):
                return ("mg", tag, k, t)

            p4_state = [0]

            def p4(c):
                def fn(s):
                    slot = k8(ring[:, s, :])
                    if c == 0:
                        alloc(64 * K1, 96 * K1, [mk_(k, t) for k in range(KD) for t in range(NT)])
                    br_rng = ((0, 4), (4, 6), (6, 8))
                    for th in range(2):
                        for b in range(3):
                            st_ = p4_state[0] % 2
                            p4_state[0] += 1
                            tiles = (2 * th, 2 * th + 1)
                            for k in range(KD):
                                for tt, t in enumerate(tiles):
                                    gb = 2 * st_ + tt
                                    MM(ps[gb][:], slot[:, k, b * 128:(b + 1) * 128], hT2[:, k, tsl(t)], k == 0, k == KD - 1,
                                       ring_keys(s) + [h2k(k, t)], [("ps", gb)])
                            k0, k1 = br_rng[b]
                            for k in range(k0, k1):
                                for tt, t in enumerate(tiles):
                                    bb = 4 + 2 * st_ + tt
                                    src = brA[:, k, tsl(t)] if k < 4 else brBC[:, k - 4, tsl(t)]
                                    MM(ps[bb][:], slot[:, k, 384:512], src, k == k0, k == k1 - 1,
                                       ring_keys(s) + [brk(k, t)], [("ps", bb)])
                            for tt, t in enumerate(tiles):
                                gb = 2 * st_ + tt
                                bb = 4 + 2 * st_ + tt
                                ti = tt
                                ACT(tmp[:, ti, :], ps[gb][:], AF.Sigmoid, [("ps", gb)], [("tmp", ti)])
                                acc = rstd[:, tsl(t)]
                                if b == 0:
                                    TT_(acc, ps[bb][:], tmp[:, ti, :], ALU.mult, [("ps", bb), ("tmp", ti)],
                                        [("rstd", t)])
                                else:
                                    TT_(tmp[:, ti, :], ps[bb][:], tmp[:, ti, :], ALU.mult,
                                        [("ps", bb), ("tmp", ti)], [("tmp", ti)])
                                    if b == 1:
                                        TT_(acc, acc, tmp[:, ti, :], ALU.add, [("rstd", t), ("tmp", ti)], [("rstd", t)])
                                    else:
                                        TT_(merged[:, c, tsl(t)], acc, tmp[:, ti, :], ALU.add,
                                            [("rstd", t), ("tmp", ti)], [mk_(c, t)])
                pieces = []
                for b in range(3):
                    pieces.append((lambda sl, b=b: k8(sl)[:, :, b * 128:(b + 1) * 128],
                                   win[:, :, C_G + b * D + c * 128:C_G + b * D + (c + 1) * 128]))
                pieces.append((lambda sl: k8(sl)[:, :, 384:512], wbr[:, :, c * 128:(c + 1) * 128]))
                return (pieces, fn)

            for c in range(KD):
                items.append(p4(c))
            if l == 0:
                items.append((None, lambda _: dump("merged", wbf(64 * K1, 32 * K1), 16384, [mk_(k, t) for k in range(KD) for t in range(NT)])))

            def yk(k, t):
                return ("y", tag, k, t)

            def p5(half):
                def fn(s):
                    if half == 0:
                        alloc(0, 64 * K1, [yk(k, t) for k in range(KD) for t in range(NT)])
                    proj_fm(k8(ring[:, s, :]), s, 4, lambda c: c * 128, merged, mk_,
                            lambda c, t, p_, pk: EVAC(y_sb[:, half * 4 + c, tsl(t)], p_, [pk], [yk(half * 4 + c, t)]),
                            None, bsc)
                return ([(k8, wout[:, :, half * 512:(half + 1) * 512])], fn)

            items.append(p5(0))
            items.append(p5(1))

            def p5b(tiles):
                def fn(_):
                    for t in tiles:
                        norm_stats(t, lambda k, t_: y_sb[:, k, tsl(t_)], yk)
                        residual_update(l, t, lambda k, t_=t: y_sb[:, k, tsl(t_)], lambda k, t_=t: yk(k, t_), 8)
                return fn

            defer_p5b = False
            if mod_in_p5b:
                for t in range(NT):
                    items.append((None, p5b((t,))))
                    for it in modl1[3 * t:3 * t + 3]:
                        items.append(it)
            else:
                items.append((None, p5b((0, 1) if defer_p5b else (0, 1, 2, 3))))
            if l == 0:
                items.append((None, lambda _: dump("xmix", x_sb[:].rearrange("p k t -> p (k t)"), 16384, [xk(k, t) for k in range(KD) for t in range(NT)])))
            if stop_after_mix:
                return

            for hf in range(2):
                ftag = "%s_f%d" % (tag, hf)
                h2T = wbf(0, 16 * K1).rearrange("p (k t) -> p k t", k=KD)
                aT = wbf(16 * K1, 64 * K1).rearrange("p (k t) -> p k t", k=32)
                y2lo = wf32(0, 16 * K1).rearrange("p (k t) -> p k t", k=4)
                y2hi = wf32(80 * K1, 16 * K1).rearrange("p (k t) -> p k t", k=4)

                def f1(_, hf=hf, ftag=ftag, h2T=h2T):
                    alloc(0, 16 * K1, [("h2", ftag, k, tl) for k in range(KD) for tl in range(2)])
                    for tl in range(2):
                        t = hf * 2 + tl
                        if hf == 0:
                            norm_stats(t, lambda k, t_: x_sb[:, k, tsl(t_)], xk, split=False)
                        modulate(l, t, lambda k, tl_=tl: h2T[:, k, tl_ * TT:(tl_ + 1) * TT],
                                 lambda k, tl_=tl: ("h2", ftag, k, tl_), 16)

                items.append((None, f1))

                def f2(g, hf=hf, ftag=ftag, h2T=h2T, aT=aT):
                    def fn(s):
                        slot = k8(ring[:, s, :])
                        if g == 0:
                            alloc(16 * K1, 80 * K1, [("a", ftag, kc, tl) for kc in range(32) for tl in range(2)])
                        for cc in range(4):
                            kc = g * 4 + cc
                            for k in range(KD):
                                for tl in range(2):
                                    bank = (kc % 4) * 2 + tl
                                    MM(ps[bank][:], slot[:, k, cc * 128:(cc + 1) * 128], h2T[:, k, tl * TT:(tl + 1) * TT],
                                       k == 0, k == KD - 1, ring_keys(s) + [("h2", ftag, k, tl)], [("ps", bank)])
                            for tl in range(2):
                                bank = (kc % 4) * 2 + tl
                                ti = tl
                                ACT(tmp[:, ti, :], ps[bank][:], AF.Relu, [("ps", bank)], [("tmp", ti)])
                                TT_(aT[:, kc, tl * TT:(tl + 1) * TT], tmp[:, ti, :], tmp[:, ti, :], ALU.mult,
                                    [("tmp", ti)], [("a", ftag, kc, tl)])
                    return ([(k8, wff1[:, :, g * 512:(g + 1) * 512])], fn)

                for g in range(8):
                    items.append(f2(g))
                    if hf == 0 and defer_p5b and g in (0, 1):
                        items.append((None, p5b((2 + g,))))

                def f3(c, hf=hf, ftag=ftag, aT=aT, y2lo=y2lo, y2hi=y2hi):
                    def fn(s):
                        slot = ring[:, s, :].rearrange("p (k n) -> p k n", k=32)
                        if c == 0:
                            alloc(0, 16 * K1, [("y2", ftag, k, tl) for k in range(4) for tl in range(2)])
                            alloc(80 * K1, 96 * K1, [("y2", ftag, k, tl) for k in range(4, 8) for tl in range(2)])
                        for kc in range(32):
                            for tl in range(2):
                                bank = (c % 4) * 2 + tl
                                MM(ps[bank][:], slot[:, kc, :], aT[:, kc, tl * TT:(tl + 1) * TT], kc == 0, kc == 31,
                                   ring_keys(s) + [("a", ftag, kc, tl)], [("ps", bank)])
                        for tl in range(2):
                            bank = (c % 4) * 2 + tl
                            dst = (y2lo if c < 4 else y2hi)[:, c % 4, tl * TT:(tl + 1) * TT]
                            EVAC(dst, ps[bank][:], [("ps", bank)], [("y2", ftag, c, tl)])
                    return ([(lambda sl: sl.rearrange("p (k n) -> p k n", k=32), wff2[:, :, c * 128:(c + 1) * 128])], fn)

                for c in range(KD):
                    items.append(f3(c))
                    if hf == 0 and c == 0:
                        def early_stats(_):
                            for t in (2, 3):
                                norm_stats(t, lambda k, t_: x_sb[:, k, tsl(t_)], xk, split=False)
                        items.append((None, early_stats))

                def f4(_, hf=hf, ftag=ftag, y2lo=y2lo, y2hi=y2hi):
                    def yf(k, tl):
                        return (y2lo if k < 4 else y2hi)[:, k % 4, tl * TT:(tl + 1) * TT]
                    for tl in range(2):
                        t = hf * 2 + tl
                        norm_stats(t, lambda k, t_, tl_=tl: yf(k, tl_), lambda k, t_, tl_=tl: ("y2", ftag, k, tl_))
                        residual_update(l, t, lambda k, tl_=tl: yf(k, tl_), lambda k, tl_=tl: ("y2", ftag, k, tl_), 24)
                        if l == n_layers - 1 and not stop_after_mix:
                            ov = out_d.rearrange("(k p) t -> p k t", p=128)
                            for k in range(KD):
                                DMA("sp", ov[:, k, tsl(t)], x_sb[:, k, tsl(t)], [xk(k, t)], (), s_out[k])

                items.append((None, f4))

        mod0 = mod_item_list(0)
        for it in mod0[0:4]:
            items.append(it)
        for l in range(n_layers):
            layer_items(l)

        def fin(_):
            ov = out_d.rearrange("(k p) t -> p k t", p=128)
            for k in range(KD):
                DMA("sp", ov[:, k, :], x_sb[:, k, :], [xk(k, t) for t in range(NT)], (), s_out[k])

        if stop_after_mix:
            items.append((None, fin))
        run_items()
        P.finalize(nc)
        build_nc.stats = P.stats
        build_nc.dbg_off = dbg_off
    return nc


def _consts():
    cons = np.zeros((128, NCON_D), np.float32)
    s = np.arange(128)
    cons[:, CON_MASK:CON_MASK + 128] = (s[None, :] >= s[:, None]).astype(np.float32)
    cons[:, CON_ID:CON_ID + 128] = np.eye(128, dtype=np.float32)
    cons[:, CON_MNEG:CON_MNEG + 128] = np.where(s[None, :] < s[:, None], MASK_NEG, 0.0).astype(np.float32)
    wins = (2, 4, 8, 16)
    for c in range(2):
        for half in range(2):
            w = wins[c * 2 + half]
            rows = slice(half * 64, half * 64 + 64)
            cons[rows, CON_INVW + c] = 1.0 / w
            cons[rows, CON_INVC + c * 16:CON_INVC + (c + 1) * 16] = 1.0 / np.minimum(np.arange(1, 17), w)
    return cons


def _fm(v):
    return np.ascontiguousarray(np.asarray(v, np.float32).reshape(-1, 128).T)


_NC_CACHE = {}


def kernel(x, c, w_ada, b_ada, g_mix_pre, g_mix_post, g_ff_pre, g_ff_post, w_in, b_f,
           w_pool, pool_scale, conv_w, w_branch, w_out, w_ff1, w_ff2):
    x = np.asarray(x, np.float32)
    n = x.shape[0]
    vecs = np.zeros((DEPTH, 128, NV), np.float32)
    for l in range(DEPTH):
        vecs[l, :, V_BADA:V_BADA + 48] = _fm(b_ada[l])
        vecs[l, :, V_GMPRE:V_GMPRE + 8] = _fm(g_mix_pre[l])
        vecs[l, :, V_GMPOST:V_GMPOST + 8] = _fm(g_mix_post[l])
        vecs[l, :, V_GFPRE:V_GFPRE + 8] = _fm(g_ff_pre[l])
        vecs[l, :, V_GFPOST:V_GFPOST + 8] = _fm(g_ff_post[l])
        vecs[l, :, V_PSCALE:V_PSCALE + 2] = _fm(pool_scale[l])
        for tap in range(3):
            vecs[l, :, V_CONVW + tap * 2:V_CONVW + tap * 2 + 2] = _fm(conv_w[l, tap])
        vecs[l, :, V_BF:V_BF + 128] = np.tile(np.asarray(b_f[l], np.float32), 16)[None, :]
    cons = _consts()
    shared = {
        "vecs": vecs, "cons": cons,
        "w_ada": np.ascontiguousarray(w_ada, np.float32), "w_in": np.ascontiguousarray(w_in, np.float32),
        "w_pool": np.ascontiguousarray(w_pool, np.float32), "w_branch": np.ascontiguousarray(w_branch, np.float32),
        "w_out": np.ascontiguousarray(w_out, np.float32), "w_ff1": np.ascontiguousarray(w_ff1, np.float32),
        "w_ff2": np.ascontiguousarray(w_ff2, np.float32),
    }
    in_maps = []
    for b in range(n):
        m = dict(shared)
        m["xT"] = np.ascontiguousarray(x[b].T)
        m["cT"] = _fm(c[b])
        in_maps.append(m)
    if "nc" not in _NC_CACHE:
        _NC_CACHE["nc"] = build_nc()
    nc = _NC_CACHE["nc"]
    res = run_bass_kernel_spmd(nc, in_maps, core_ids=list(range(n)))
    out = np.stack([np.ascontiguousarray(r["outT"].T) for r in res.results], axis=0)
    return out.astype(np.float32)
```

```python
import numpy as np
from contextlib import ExitStack
import concourse.bass as bass
import concourse.mybir as mybir
from concourse.bass_utils import run_bass_kernel_spmd

F32 = mybir.dt.float32
BF16 = mybir.dt.bfloat16
AF = mybir.ActivationFunctionType
ALU = mybir.AluOpType

D = 1024
S = 2048
KD = 8
TT = 512
NT = 4
DEPTH = 2
DFF = 4096
IN_COLS = 5640
C_Q, C_K, C_V, C_F, C_PU, C_CH, C_CB, C_CC, C_G = 0, 512, 1024, 1536, 1544, 1800, 2056, 2312, 2568
NV = 216
V_BADA, V_GMPRE, V_GMPOST, V_GFPRE, V_GFPOST, V_PSCALE, V_CONVW, V_BF = 0, 48, 56, 64, 72, 80, 82, 88
NCON = 162
NCON_D = 418
CON_ID, CON_MNEG = 162, 290
MASK_NEG = -30000.0
CON_MASK, CON_INVW, CON_INVC = 0, 128, 130
RING = 3
RMS_EPS = 1e-6
ENGS = ("pe", "act", "dve", "pool", "sp")


class _Op:
    __slots__ = ("fn", "deps", "marked", "dma", "waits")

    def __init__(self, fn, dma):
        self.fn = fn
        self.dma = dma
        self.deps = []
        self.marked = False
        self.waits = []


class Prog:
    def __init__(self):
        self.ops = {e: [] for e in ENGS}
        self.last_w = {}
        self.readers = {}
        self.n_dma_sems = 0
        self.dma_cnt = {}

    def new_dma_sem(self):
        i = self.n_dma_sems
        self.n_dma_sems += 1
        self.dma_cnt[i] = 0
        return i

    def op(self, eng, fn, reads=(), writes=(), dma_sem=None):
        lst = self.ops[eng]
        seq = len(lst)
        dma = None
        if dma_sem is not None:
            self.dma_cnt[dma_sem] += 16
            dma = (dma_sem, self.dma_cnt[dma_sem])
            tok = ("d", dma_sem, dma[1])
        else:
            tok = ("c", eng, seq)
        o = _Op(fn, dma)
        raw = set()
        wdeps = set()
        for k in reads:
            t = self.last_w.get(k)
            if t is not None:
                raw.add(t)
        for k in writes:
            t = self.last_w.get(k)
            if t is not None:
                wdeps.add(t)
            for t in self.readers.get(k, ()):
                wdeps.add(t)
        for t in raw:
            if t[0] == "c" and t[1] == eng and dma is None and eng == "pe":
                continue
            o.deps.append(t)
        for t in wdeps - raw:
            if t[0] == "c" and t[1] == eng and dma is None:
                continue
            o.deps.append(t)
        for k in reads:
            self.readers.setdefault(k, []).append(tok)
        for k in writes:
            self.last_w[k] = tok
            self.readers[k] = []
        lst.append(o)
        return tok

    def handoff(self, dead_keys, new_keys):
        best = {}
        for k in dead_keys:
            toks = list(self.readers.get(k, ()))
            t = self.last_w.get(k)
            if t is not None:
                toks.append(t)
            for t in toks:
                key = (t[0], t[1])
                if key not in best or best[key][2] < t[2]:
                    best[key] = t
        toks = list(best.values())
        for nk in new_keys:
            self.readers.setdefault(nk, []).extend(toks)

    def finalize(self, nc):
        for e in ENGS:
            obs = {}
            for o in self.ops[e]:
                best = {}
                for t in o.deps:
                    key = (t[0], t[1])
                    if obs.get(key, -1) >= t[2]:
                        continue
                    if key not in best or best[key][2] < t[2]:
                        best[key] = t
                o.waits = list(best.values())
                for t in o.waits:
                    obs[(t[0], t[1])] = t[2]
                    if t[0] == "c":
                        self.ops[t[1]][t[2]].marked = True
        cum = {}
        for e in ENGS:
            c = 0
            arr = []
            for o in self.ops[e]:
                if o.marked:
                    c += 1
                arr.append(c)
            cum[e] = arr
        self.stats = {e: (len(self.ops[e]), cum[e][-1] if cum[e] else 0,
                          sum(len(o.waits) for o in self.ops[e])) for e in ENGS}
        with ExitStack() as st:
            esem = {e: st.enter_context(nc.semaphore("s_" + e)) for e in ENGS}
            dsem = [st.enter_context(nc.semaphore("d_%d" % i)) for i in range(self.n_dma_sems)]
            block = st.enter_context(nc.Block())

            def run(e, eng):
                for o in self.ops[e]:
                    for t in o.waits:
                        if t[0] == "c":
                            eng.wait_ge(esem[t[1]], cum[t[1]][t[2]])
                        else:
                            eng.wait_ge(dsem[t[1]], t[2])
                    ins = o.fn(eng)
                    if o.dma is not None:
                        ins.then_inc(dsem[o.dma[0]], 16)
                    elif o.marked:
                        ins.then_inc(esem[e], 1)

            @block.tensor
            def _(eng):
                run("pe", eng)

            @block.scalar
            def _(eng):
                run("act", eng)

            @block.vector
            def _(eng):
                run("dve", eng)

            @block.gpsimd
            def _(eng):
                run("pool", eng)

            @block.sync
            def _(eng):
                run("sp", eng)
                for i in range(self.n_dma_sems):
                    if self.dma_cnt[i] > 0:
                        eng.wait_ge(dsem[i], self.dma_cnt[i])


def build_nc(dbg=False, n_layers=DEPTH, stop_after_mix=False):
    nc = bass.Bass("TRN2", target_bir_lowering=False)
    DBGN = 16 * 16384
    dbg_d = nc.dram_tensor("dbg", [128, DBGN], F32, kind="ExternalOutput").ap() if dbg else None
    xT_d = nc.dram_tensor("xT", [D, S], F32, kind="ExternalInput").ap()
    cT_d = nc.dram_tensor("cT", [128, KD], F32, kind="ExternalInput").ap()
    vecs_d = nc.dram_tensor("vecs", [DEPTH, 128, NV], F32, kind="ExternalInput").ap()
    cons_d = nc.dram_tensor("cons", [128, NCON_D], F32, kind="ExternalInput").ap()
    w_ada_d = nc.dram_tensor("w_ada", [DEPTH, D, 6 * D], F32, kind="ExternalInput").ap()
    w_in_d = nc.dram_tensor("w_in", [DEPTH, D, IN_COLS], F32, kind="ExternalInput").ap()
    w_pool_d = nc.dram_tensor("w_pool", [DEPTH, 4, 64, 64], F32, kind="ExternalInput").ap()
    w_br_d = nc.dram_tensor("w_branch", [DEPTH, D, D], F32, kind="ExternalInput").ap()
    w_out_d = nc.dram_tensor("w_out", [DEPTH, D, D], F32, kind="ExternalInput").ap()
    w_ff1_d = nc.dram_tensor("w_ff1", [DEPTH, D, DFF], F32, kind="ExternalInput").ap()
    w_ff2_d = nc.dram_tensor("w_ff2", [DEPTH, DFF, D], F32, kind="ExternalInput").ap()
    out_d = nc.dram_tensor("outT", [D, S], F32, kind="ExternalOutput").ap()

    st = ExitStack()
    with st:
        def sb(name, shape, dt):
            return st.enter_context(nc.sbuf_tensor(name, shape, dt))

        x_sb = sb("x_sb", [128, KD, S], F32)
        ring = sb("ring", [128, RING, 4096], BF16)
        rstd = sb("rstd", [128, S], F32)
        Wt = sb("W", [128, 49152], BF16)
        vec_sb = sb("vec_sb", [128, DEPTH, NV], F32)
        cons_sb = sb("cons_sb", [128, NCON], F32)
        ident_bf = sb("ident_bf", [128, 128], BF16)
        mneg_bf = sb("mneg_bf", [128, 128], BF16)
        ones_bf = sb("ones_bf", [128, 128], BF16)
        ones_f = sb("ones_f", [128, 128], F32)
        mod_sb = sb("mod_sb", [128, DEPTH, 48], F32)
        der = sb("der", [128, DEPTH, 32], F32)
        c_sb = sb("c_sb", [128, KD], F32)
        sc_bf = sb("sc_bf", [128, KD], BF16)
        wf_bf = sb("wf_bf", [128, KD, 8], BF16)
        wpool_bf = sb("wpool_bf", [128, 4, 128], BF16)
        sq = sb("sq", [128, 2, TT], BF16)
        tmp = sb("tmp", [128, 2, TT], F32)
        eps_t = sb("eps_t", [128, 1], F32)
        sp_t = sb("sp_t", [128, 128], F32)
        fsc = sb("fsc", [128, 4, 128], F32)
        Gh = sb("Gh", [128, 8, 16], F32)
        ps = [st.enter_context(nc.psum_tensor("ps%d" % i, [128, TT], F32)) for i in range(8)]

        P = Prog()
        s_misc = [P.new_dma_sem() for _ in range(4)]
        s_x = [P.new_dma_sem() for _ in range(NT)]
        s_out = [P.new_dma_sem() for _ in range(KD)]
        s_ring = [P.new_dma_sem() for _ in range(RING)]
        s_mask = P.new_dma_sem()
        s_mask2 = P.new_dma_sem()
        s_wp = [P.new_dma_sem() for _ in range(8)]
        s_wf = P.new_dma_sem()

        def MM(out, lhsT, rhs, start, stop, reads, writes):
            P.op("pe", lambda e: e.matmul(out, lhsT=lhsT, rhs=rhs, start=start, stop=stop), reads, writes)

        def ACT(out, in_, func, reads, writes, bias=None, scale=None):
            kw = {}
            if bias is not None:
                kw["bias"] = bias
            if scale is not None:
                kw["scale"] = scale
            P.op("act", lambda e: e.activation(out=out, in_=in_, func=func, **kw), reads, writes)

        def TT_(out, in0, in1, op, reads, writes):
            P.op("dve", lambda e: e.tensor_tensor(out=out, in0=in0, in1=in1, op=op), reads, writes)

        def TS(out, in0, s1, s2, op0, op1, reads, writes):
            if op1 is None:
                P.op("dve", lambda e: e.tensor_scalar(out=out, in0=in0, scalar1=s1, scalar2=None, op0=op0),
                     reads, writes)
            else:
                P.op("dve", lambda e: e.tensor_scalar(out=out, in0=in0, scalar1=s1, scalar2=s2, op0=op0, op1=op1),
                     reads, writes)

        def STT(out, in0, scalar, in1, op0, op1, reads, writes):
            P.op("dve", lambda e: e.scalar_tensor_tensor(out=out, in0=in0, scalar=scalar, in1=in1, op0=op0, op1=op1),
                 reads, writes)

        def VCOPY(out, in_, reads, writes):
            P.op("dve", lambda e: e.tensor_copy(out=out, in_=in_), reads, writes)

        def MEMSET(ap, val, writes):
            P.op("dve", lambda e: e.memset(ap, val), (), writes)

        def RECIP(out, in_, reads, writes):
            P.op("dve", lambda e: e.reciprocal(out=out, in_=in_), reads, writes)

        def DMA(q, out, in_, reads, writes, sem):
            P.op(q, lambda e: e.dma_start(out=out, in_=in_), reads, writes, dma_sem=sem)

        dbg_off = {}
        dbg_sems = []

        def dump(name, view2d, n, reads):
            if not dbg:
                return
            o = len(dbg_off) * 16384
            dbg_off[name] = (o, n)
            sem = P.new_dma_sem()
            DMA("pool", dbg_d[:, o:o + n], view2d, reads, (), sem)

        evac_ctr = [0]

        def EVAC(out, in_, reads, writes):
            evac_ctr[0] += 1
            if evac_ctr[0] % 2 == 0:
                ACT(out, in_, AF.Copy, reads, writes)
            else:
                VCOPY(out, in_, reads, writes)

        arena = []

        def alloc(lo, hi, keys):
            dead = []
            keep = []
            for ent in arena:
                if ent[0] < hi and lo < ent[1]:
                    dead.extend(ent[2])
                else:
                    keep.append(ent)
            arena[:] = keep
            if dead:
                P.handoff(dead, keys)
            arena.append((lo, hi, list(keys)))

        def wbf(lo_bytes, nbytes):
            return Wt[:, lo_bytes // 2:(lo_bytes + nbytes) // 2]

        def wf32(lo_bytes, nbytes):
            return Wt[:, lo_bytes // 2:(lo_bytes + nbytes) // 2].bitcast(F32)

        K1 = 1024

        ring_state = {"n": 0}

        def ring_keys(s):
            return [("ring", s, i) for i in range(4)]

        def ring_load(pieces):
            s = ring_state["n"] % RING
            ring_state["n"] += 1
            n = len(pieces)
            for i, (dst_fn, src) in enumerate(pieces):
                wk = [("ring", s, i)] if i < n - 1 else [("ring", s, j) for j in range(i, 4)]
                DMA("pool", dst_fn(ring[:, s, :]), src, (), wk, s_ring[s])
            final_tok = ("d", s_ring[s], P.dma_cnt[s_ring[s]])
            for kk in ring_keys(s):
                P.last_w[kk] = final_tok
            return s

        items = []

        def run_items():
            widx = [i for i, it in enumerate(items) if it[0] is not None]
            slot_of = {}
            loaded = 0
            done_w = 0
            for i, (pieces, fn) in enumerate(items):
                while loaded < len(widx) and loaded < done_w + RING:
                    j = widx[loaded]
                    slot_of[j] = ring_load(items[j][0])
                    loaded += 1
                fn(slot_of.get(i))
                if pieces is not None:
                    done_w += 1

        def xk(k, t):
            return ("x", k, t)

        def tsl(t):
            return slice(t * TT, (t + 1) * TT)

        def setup(_):
            DMA("sp", cons_sb[:], cons_d[:, 0:NCON], (), ["cons"], s_misc[0])
            DMA("sp", c_sb[:], cT_d, (), ["c"], s_misc[1])
            for l in range(DEPTH):
                DMA("sp", vec_sb[:, l, :], vecs_d[l], (), [("vec", l)], s_misc[2 + l])
            xv = xT_d.rearrange("(k p) t -> p k t", p=128)
            for t in range(NT):
                for k in range(KD):
                    DMA("sp", x_sb[:, k, tsl(t)], xv[:, k, tsl(t)], (), [xk(k, t)], s_x[t])
                ftok = ("d", s_x[t], P.dma_cnt[s_x[t]])
                for k in range(KD):
                    P.last_w[xk(k, t)] = ftok
            DMA("pool", ident_bf[:], cons_d[:, CON_ID:CON_ID + 128], (), ["ident"], s_mask)
            DMA("pool", mneg_bf[:], cons_d[:, CON_MNEG:CON_MNEG + 128], (), ["mneg"], s_mask2)
            MEMSET(ones_bf[:], 1.0, ["ones_bf"])
            MEMSET(ones_f[:], 1.0, ["ones_f"])
            MEMSET(eps_t[:], RMS_EPS, ["eps"])
            MEMSET(wpool_bf[:], 0.0, [("wpool", l, g) for l in range(DEPTH) for g in range(4)])
            for l in range(DEPTH):
                for g in range(4):
                    r0 = (g % 2) * 64
                    DMA("pool", wpool_bf[r0:r0 + 64, l * 2 + g // 2, r0:r0 + 64], w_pool_d[l, g], (),
                        [("wpool", l, g)], s_wp[l * 4 + g])
            ACT(sc_bf[:], c_sb[:], AF.Silu, ["c"], ["sc"])

        items.append((None, setup))

        def mod_item_list(l):
            wv = w_ada_d[l].rearrange("(k p) n -> p k n", p=128)
            out = []

            def mk(g):
                def fn(s):
                    slot = ring[:, s, :].rearrange("p (k n) -> p k n", k=KD)
                    mbank = 2 + g % 6
                    pm = ps[mbank]
                    ml = mod_sb[:, l, :]
                    dl = der[:, l, :]
                    vl = vec_sb[:, l, :]
                    mk_, dk_ = ("mod", l), ("der", l)
                    for jj in range(4):
                        for k in range(KD):
                            MM(pm[:, jj:jj + 1], slot[:, k, jj * 128:(jj + 1) * 128], sc_bf[:, k:k + 1],
                               k == 0, k == KD - 1, ring_keys(s) + ["sc"], [("ps", mbank)])
                    TT_(ml[:, g * 4:(g + 1) * 4], pm[:, 0:4], vl[:, V_BADA + g * 4:V_BADA + (g + 1) * 4], ALU.add,
                        [("ps", mbank), ("vec", l)], [mk_])
                    if g == 3:
                        STT(dl[:, 0:8], ml[:, 8:16], 1.0, vl[:, V_GMPRE:V_GMPRE + 8], ALU.add, ALU.mult,
                            [mk_, ("vec", l)], [dk_])
                    if g == 5:
                        TT_(dl[:, 8:16], ml[:, 16:24], vl[:, V_GMPOST:V_GMPOST + 8], ALU.mult, [mk_, ("vec", l)], [dk_])
                    if g == 9:
                        STT(dl[:, 16:24], ml[:, 32:40], 1.0, vl[:, V_GFPRE:V_GFPRE + 8], ALU.add, ALU.mult,
                            [mk_, ("vec", l)], [dk_])
                    if g == 11:
                        TT_(dl[:, 24:32], ml[:, 40:48], vl[:, V_GFPOST:V_GFPOST + 8], ALU.mult, [mk_, ("vec", l)], [dk_])
                pieces = [(lambda sl: sl.rearrange("p (k n) -> p k n", k=KD), wv[:, :, g * 512:(g + 1) * 512])]
                return (pieces, fn)

            for g in range(12):
                out.append(mk(g))
            return out

        def norm_stats(t, src_fn, src_keys_fn, split=False):
            bank = t % 2
            for k in range(KD):
                if split and k % 2 == 1:
                    TT_(sq[:, k % 2, :], src_fn(k, t), src_fn(k, t), ALU.mult, [src_keys_fn(k, t)], [("sq", k % 2)])
                else:
                    ACT(sq[:, k % 2, :], src_fn(k, t), AF.Square, [src_keys_fn(k, t)], [("sq", k % 2)])
                MM(ps[bank][:], ones_bf[:], sq[:, k % 2, :], k == 0, k == KD - 1,
                   ["ones_bf", ("sq", k % 2)], [("ps", bank)])
            ACT(tmp[:, 0, :], ps[bank][:], AF.Ln, [("ps", bank), "eps"], [("tmp", 0)],
                bias=eps_t[:], scale=1.0 / D)
            ACT(rstd[:, tsl(t)], tmp[:, 0, :], AF.Exp, [("tmp", 0)], [("rstd", t)], scale=-0.5)

        def modulate(l, t, dst_fn, dst_key_fn, col0):
            sh0 = 0 if col0 == 0 else 24
            for k in range(KD):
                TT_(tmp[:, k % 2, :], x_sb[:, k, tsl(t)], rstd[:, tsl(t)], ALU.mult,
                    [xk(k, t), ("rstd", t)], [("tmp", k % 2)])
                if False:
                    TS(dst_fn(k), tmp[:, k % 2, :], der[:, l, col0 + k:col0 + k + 1], mod_sb[:, l, sh0 + k:sh0 + k + 1],
                       ALU.mult, ALU.add, [("tmp", k % 2), ("der", l), ("mod", l)], [dst_key_fn(k)])
                else:
                    ACT(dst_fn(k), tmp[:, k % 2, :], AF.Identity, [("tmp", k % 2), ("der", l), ("mod", l)],
                        [dst_key_fn(k)], bias=mod_sb[:, l, sh0 + k:sh0 + k + 1],
                        scale=der[:, l, col0 + k:col0 + k + 1])

        def residual_update(l, t, y_fn, y_key_fn, gcol0):
            for k in range(KD):
                TT_(tmp[:, k % 2, :], y_fn(k), rstd[:, tsl(t)], ALU.mult, [y_key_fn(k), ("rstd", t)],
                    [("tmp", k % 2)])
                STT(x_sb[:, k, tsl(t)], tmp[:, k % 2, :], der[:, l, gcol0 + k:gcol0 + k + 1], x_sb[:, k, tsl(t)],
                    ALU.mult, ALU.add, [("tmp", k % 2), ("der", l), xk(k, t)], [xk(k, t)])

        def layer_items(l):
            win = w_in_d[l].rearrange("(k p) n -> p k n", p=128)
            wbr = w_br_d[l].rearrange("(k p) n -> p k n", p=128)
            wout = w_out_d[l].rearrange("(k p) n -> p k n", p=128)
            wff1 = w_ff1_d[l].rearrange("(k p) n -> p k n", p=128)
            wff2 = w_ff2_d[l].rearrange("(k p) n -> p k n", p=128)
            tag = "L%d" % l

            def k8(sl):
                return sl.rearrange("p (k n) -> p k n", k=KD)

            hT = wbf(0, 32 * K1).rearrange("p (k t) -> p k t", k=KD)
            qT = wbf(32 * K1, 16 * K1).rearrange("p (c t) -> p c t", c=4)
            kT = wbf(48 * K1, 16 * K1).rearrange("p (c t) -> p c t", c=4)
            V_sb = wbf(64 * K1, 24 * K1).rearrange("p (a b c) -> p a b c", a=16, b=4)
            bias_sb = wf32(88 * K1, 2 * K1).rearrange("p (h g j) -> p h g j", h=8, g=4)
            brA = wbf(0, 16 * K1).rearrange("p (c t) -> p c t", c=4)
            PT = wbf(16 * K1, 8 * K1).rearrange("p (n t) -> p n t", n=8)
            hT2 = wbf(32 * K1, 32 * K1).rearrange("p (k t) -> p k t", k=KD)
            temps = wf32(64 * K1, 32 * K1).rearrange("p (n t) -> p n t", n=4)
            pbf = wbf(64 * K1 + 8 * K1, 4 * K1)
            brBC = wbf(16 * K1, 16 * K1).rearrange("p (c t) -> p c t", c=4)
            merged = wbf(64 * K1, 32 * K1).rearrange("p (k t) -> p k t", k=KD)
            y_sb = wf32(0, 64 * K1).rearrange("p (k t) -> p k t", k=KD)

            def hk(k, t):
                return ("hT", tag, k, t)

            def h2k(k, t):
                return ("hT2", tag, k, t)

            def p1(_):
                alloc(0, 32 * K1, [hk(k, t) for k in range(KD) for t in range(NT)])
                if l == 0:
                    for t in range(NT):
                        norm_stats(t, lambda k, t_: x_sb[:, k, tsl(t_)], xk, split=False)
                    for t in range(NT):
                        modulate(l, t, lambda k, t_=t: hT[:, k, tsl(t_)], lambda k, t_=t: hk(k, t_), 0)
                else:
                    for t in range(NT):
                        norm_stats(t, lambda k, t_: x_sb[:, k, tsl(t_)], xk, split=False)
                        modulate(l, t, lambda k, t_=t: hT[:, k, tsl(t_)], lambda k, t_=t: hk(k, t_), 0)
                DMA("pool", wf_bf[:], win[:, :, C_F:C_F + 8], (), ["wf"], s_wf)

            items.append((None, p1))
            if l == 0:
                items.append((None, lambda _: dump("hT", wbf(0, 32 * K1), 16384, [hk(k, t) for k in range(KD) for t in range(NT)])))

            def vkey(tt):
                return ("V", tag, tt)

            def p3v(s):
                slot = k8(ring[:, s, :])
                alloc(64 * K1, 88 * K1, [vkey(tt) for tt in range(16)])
                alloc(88 * K1, 96 * K1, [("bias", tag)])
                MEMSET(V_sb[:, :, :, 64:128], 1.0, [vkey(tt) for tt in range(16)])
                for tt in range(16):
                    bank = 2 + tt % 2
                    t = tt // 4
                    for k in range(KD):
                        lhs = hT[:, k, tt * 128:(tt + 1) * 128]
                        MM(ps[bank][:], lhs, slot[:, k, :], k == 0, k == KD - 1,
                           [hk(k, t)] + ring_keys(s), [("ps", bank)])
                        MM(ps[6][:, tt * 8:(tt + 1) * 8], lhs, wf_bf[:, k, :], k == 0, k == KD - 1,
                           [hk(k, t), "wf"], [("ps", 6)])
                    pv = ps[bank][:].rearrange("p (a b) -> p a b", b=128)
                    ACT(V_sb[:, tt, :, 0:64], pv[:, :, 0:64], AF.Copy, [("ps", bank)], [vkey(tt)])
                    VCOPY(V_sb[:, tt, :, 128:192], pv[:, :, 64:128], [("ps", bank)], [vkey(tt)])
                TT_(sp_t[:], ps[6][:, 0:128], vec_sb[:, l, V_BF:V_BF + 128], ALU.add, [("ps", 6), ("vec", l)], ["sp"])
                ACT(sp_t[:], sp_t[:], AF.Exp, ["sp"], ["sp"], scale=-1.0)
                ACT(sp_t[:], sp_t[:], AF.Ln, ["sp"], ["sp"], bias=1.0)
                MM(ps[7][:, 0:128], cons_sb[:, CON_MASK:CON_MASK + 128], sp_t[:], True, True,
                   ["cons", "sp"], [("ps", 7)])
                MM(ps[7][:, 128:256], ones_f[:], sp_t[:], True, True, ["ones_f", "sp"], [("ps", 7)])
                VCOPY(fsc[:, 0, :], ps[7][:, 128:256], [("ps", 7)], ["fsc"])
                MEMSET(fsc[:, 1, 0:8], 0.0, ["fsc"])
                for i in range(1, 16):
                    TT_(fsc[:, 1, i * 8:(i + 1) * 8], fsc[:, 1, (i - 1) * 8:i * 8], fsc[:, 0, (i - 1) * 8:i * 8],
                        ALU.add, ["fsc"], ["fsc"])
                TT_(fsc[:, 2, :], fsc[:, 1, :], fsc[:, 0, :], ALU.add, ["fsc"], ["fsc"])
                TT_(Gh[:].rearrange("p h j -> p j h"), ps[7][:, 0:128].rearrange("p (j h) -> p j h", h=8),
                    fsc[:, 1, :].rearrange("p (j h) -> p j h", h=8), ALU.add, [("ps", 7), "fsc"], ["Gh"])
                for g in range(NT):
                    TT_(fsc[:, 3, g * 8:(g + 1) * 8], fsc[:, 1, (4 * g) * 8:(4 * g) * 8 + 8],
                        fsc[:, 2, (4 * g + 3) * 8:(4 * g + 3) * 8 + 8], ALU.add, ["fsc"], ["fsc"])
                TS(fsc[:, 3, 0:32], fsc[:, 3, 0:32], 0.5, None, ALU.mult, None, ["fsc"], ["fsc"])
                for h in range(8):
                    for g in range(NT):
                        TS(bias_sb[:, h, g, :], Gh[:, h, :], fsc[:, 3, g * 8 + h:g * 8 + h + 1], None,
                           ALU.subtract, None, ["Gh", "fsc"], [("bias", tag)])

            items.append(([(k8, win[:, :, C_V:C_V + 512])], p3v))

            def proj_fm(slot, s, ncols_chunks, col_of_chunk, rhs, rhs_key, dst_fn, dst_key_fn, bankset_ctr):
                for c in range(ncols_chunks):
                    bs = (bankset_ctr[0] % 2) * 4
                    bankset_ctr[0] += 1
                    for k in range(KD):
                        for t in range(NT):
                            MM(ps[bs + t][:], slot[:, k, col_of_chunk(c):col_of_chunk(c) + 128], rhs[:, k, tsl(t)],
                               k == 0, k == KD - 1, ring_keys(s) + [rhs_key(k, t)], [("ps", bs + t)])
                    for t in range(NT):
                        dst_fn(c, t, ps[bs + t][:], ("ps", bs + t))

            bsc = [0]

            def qkey(c, t):
                return ("q", tag, c, t)

            def kkey(c, t):
                return ("k", tag, c, t)

            def p3q(s):
                alloc(32 * K1, 48 * K1, [qkey(c, t) for c in range(4) for t in range(NT)])
                proj_fm(k8(ring[:, s, :]), s, 4, lambda c: c * 128, hT, hk,
                        lambda c, t, p_, pk: EVAC(qT[:, c, tsl(t)], p_, [pk], [qkey(c, t)]), None, bsc)

            def p3k(s):
                alloc(48 * K1, 64 * K1, [kkey(c, t) for c in range(4) for t in range(NT)])
                proj_fm(k8(ring[:, s, :]), s, 4, lambda c: c * 128, hT, hk,
                        lambda c, t, p_, pk: EVAC(kT[:, c, tsl(t)], p_, [pk], [kkey(c, t)]), None, bsc)

            items.append(([(k8, win[:, :, C_Q:C_Q + 512])], p3q))
            items.append(([(k8, win[:, :, C_K:C_K + 512])], p3k))
            if l == 0:
                def dmp1(_):
                    dump("Gh", Gh[:].rearrange("p h j -> p (h j)"), 128, ["Gh"])
                    dump("fsc", fsc[:].rearrange("p a b -> p (a b)"), 512, ["fsc"])
                    dump("bias", wf32(88 * K1, 2 * K1), 512, [("bias", tag)])
                    dump("V", wbf(64 * K1, 24 * K1), 12288, [vkey(tt) for tt in range(16)])
                    dump("q", wbf(32 * K1, 16 * K1), 8192, [qkey(c, t) for c in range(4) for t in range(NT)])
                    dump("k", wbf(48 * K1, 16 * K1), 8192, [kkey(c, t) for c in range(4) for t in range(NT)])
                items.append((None, dmp1))

            def brk(c, t):
                return ("br", tag, c, t)

            steps = []
            for h in range(8):
                for g in range(NT):
                    for j in range(4 * g + 4):
                        steps.append((h, g, j))
            BATCH = 3
            NFILL = 3
            FILL_N = 448
            batches = [list(range(i, min(i + BATCH, len(steps)))) for i in range(0, len(steps), BATCH)]
            att_state = {"m": 0}

            def sbank(n):
                return (n // BATCH % 2) * BATCH + n % BATCH

            def emit_S_batch(m):
                ns = batches[m]
                set_banks = [("ps", (m % 2) * BATCH + i) for i in range(BATCH)]
                first = True
                for _i in range(NFILL):
                    MM(ps[(m % 2) * BATCH][:, 0:FILL_N], ones_bf[:], qT[:, 0, 0:FILL_N], True, True, ["ones_bf", qkey(0, 0)],
                       set_banks if first else [set_banks[0]])
                    first = False
                for n in ns:
                    h, g, j = steps[n]
                    p, hh = h // 2, h % 2
                    pr = slice(hh * 64, hh * 64 + 64)
                    q0 = max(g * TT, j * 128)
                    off = q0 - g * TT
                    b = sbank(n)
                    diag = j >= 4 * g
                    MM(ps[b][:, off:TT], kT[pr, p, j * 128:(j + 1) * 128], qT[pr, p, q0:(g + 1) * TT], True, not diag,
                       [kkey(p, j // 4), qkey(p, g)], set_banks if first else [("ps", b)])
                    first = False
                    if diag:
                        MM(ps[b][:, off:off + 128], ident_bf[:], mneg_bf[:], False, True, ["ident", "mneg"], [("ps", b)])
                for n in ns:
                    h, g, j = steps[n]
                    q0 = max(g * TT, j * 128)
                    off = q0 - g * TT
                    b = sbank(n)
                    pt = n % (2 * BATCH)
                    ACT(PT[:, pt, off:TT], ps[b][:, off:TT], AF.Exp, [("ps", b), ("bias", tag)],
                        [("PT", tag, pt)], bias=bias_sb[:, h, g, j:j + 1], scale=0.125)

            def emit_PV_batch(m):
                ns = batches[m]
                all_pt = [("PT", tag, n % (2 * BATCH)) for n in ns]
                first = True
                for n in ns:
                    h, g, j = steps[n]
                    p, hh = h // 2, h % 2
                    pr = slice(hh * 64, hh * 64 + 64)
                    lr = slice((1 - hh) * 64, (1 - hh) * 64 + 64)
                    q0 = max(g * TT, j * 128)
                    off = q0 - g * TT
                    pt = n % (2 * BATCH)
                    gi = h * NT + g
                    ob = 6 + gi % 2
                    MM(ps[ob][:, off:TT], V_sb[:, j, p, hh * 64:hh * 64 + 128], PT[:, pt, off:TT],
                       j == 0, j == 4 * g + 3, [vkey(j)] + (all_pt if first else [("PT", tag, pt)]), [("ps", ob)])
                    first = False
                    if j == 4 * g + 3:
                        ti = gi % 2
                        RECIP(tmp[lr, ti, :], ps[ob][lr, :], [("ps", ob)], [("tmp", ti)])
                        TT_(brA[pr, p, tsl(g)], ps[ob][pr, :], tmp[lr, ti, :], ALU.mult,
                            [("ps", ob), ("tmp", ti)], [brk(p, g)])

            def p3att(h):
                def fn(_):
                    if h == 0:
                        alloc(0, 16 * K1, [brk(c, t) for c in range(4) for t in range(NT)])
                        alloc(16 * K1, 24 * K1, [("PT", tag, i) for i in range(8)])
                    nb = len(batches)
                    hi_m = ((h + 1) * nb) // 8 + (1 if h == 7 else 0)
                    while att_state["m"] < hi_m:
                        m = att_state["m"]
                        if m < nb:
                            emit_S_batch(m)
                        if m >= 1:
                            emit_PV_batch(m - 1)
                        att_state["m"] += 1
                return fn

            modl1 = mod_item_list(1) if (l == 0 and n_layers > 1) else []
            mod_in_p5b = bool(modl1) and not (stop_after_mix or dbg)
            for h in range(8):
                items.append((None, p3att(h)))
                if not mod_in_p5b:
                    for it in modl1[2 * h:2 * h + 2]:
                        items.append(it)
            if l == 0:
                items.append((None, lambda _: dump("brA", wbf(0, 16 * K1), 8192, [brk(c, t) for c in range(4) for t in range(NT)])))

            def p1b(_):
                alloc(32 * K1, 64 * K1, [h2k(k, t) for k in range(KD) for t in range(NT)])
                for t in range(NT):
                    modulate(l, t, lambda k, t_=t: hT2[:, k, tsl(t_)], lambda k, t_=t: h2k(k, t_), 0)

            items.append((None, p1b))
            if l == 0:
                for it in mod0[4:12]:
                    items.append(it)

            def p2(c):
                def fn(s):
                    slot = k8(ring[:, s, :])
                    if c == 0:
                        alloc(64 * K1, 96 * K1, [("tp", tag, n, t) for n in range(4) for t in range(NT)])
                        alloc(16 * K1, 32 * K1, [brk(4 + cc, t) for cc in range(4) for t in range(NT)])
                    proj_fm(slot, s, 4, lambda n: n * 128, hT2, h2k,
                            lambda n, t, p_, pk: EVAC(temps[:, n, tsl(t)], p_, [pk], [("tp", tag, n, t)]), None, bsc)
                    ch, cb, cc_, pu = temps[:, 0, :], temps[:, 1, :], temps[:, 2, :], temps[:, 3, :]

                    def tk(n):
                        return [("tp", tag, n, t) for t in range(NT)]
                    cw = lambda tap: vec_sb[:, l, V_CONVW + tap * 2 + c:V_CONVW + tap * 2 + c + 1]
                    TT_(cc_, cc_, ch, ALU.mult, tk(2) + tk(0), tk(2))
                    TS(ch, cc_, cw(2), None, ALU.mult, None, tk(2) + [("vec", l)], tk(0))
                    STT(ch[:, 2:S], cc_[:, 0:S - 2], cw(0), ch[:, 2:S], ALU.mult, ALU.add, tk(2) + tk(0) + [("vec", l)], tk(0))
                    STT(ch[:, 1:S], cc_[:, 0:S - 1], cw(1), ch[:, 1:S], ALU.mult, ALU.add, tk(2) + tk(0) + [("vec", l)], tk(0))
                    TT_(brBC[:, 2 + c, :], ch, cb, ALU.mult, tk(0) + tk(1), [brk(6 + c, t) for t in range(NT)])
                    nsteps_lo, nsteps_hi = (1, 2) if c == 0 else (3, 4)
                    src = pu
                    bufs = [ch, cc_]
                    srck = tk(3)
                    bk = [tk(0), tk(2)]
                    bi = 0
                    lo = slice(0, 64)
                    hi = slice(64, 128)
                    allp = slice(0, 128)
                    for step in range(nsteps_hi):
                        sh = 1 << step
                        prs = allp if step < nsteps_lo else hi
                        dst, dk = bufs[bi], bk[bi]
                        TT_(dst[prs, sh:S], src[prs, sh:S], src[prs, 0:S - sh], ALU.add, srck, dk)
                        VCOPY(dst[prs, 0:sh], src[prs, 0:sh], srck, dk)
                        if step == nsteps_lo - 1 and nsteps_lo < nsteps_hi:
                            lo_src, lo_k = dst, dk
                        src, srck = dst, dk
                        bi = 1 - bi
                    hi_src, hi_k = src, srck
                    if nsteps_lo == nsteps_hi:
                        lo_src, lo_k = src, srck
                    invw = cons_sb[:, CON_INVW + c:CON_INVW + c + 1]
                    invc = cons_sb[:, CON_INVC + c * 16:CON_INVC + (c + 1) * 16]
                    pk_ = [("tp", tag, 1, t) for t in range(NT)]
                    for prs, ssrc, sk in ((lo, lo_src, lo_k), (hi, hi_src, hi_k)):
                        STT(pbf[prs, :], ssrc[prs, :], invw[prs, :], pu[prs, :], ALU.mult, ALU.subtract,
                            sk + tk(3) + ["cons"], pk_)
                        TT_(tmp[prs, 0, 0:16], ssrc[prs, 0:16], invc[prs, :], ALU.mult, sk + ["cons"], [("tmp", 0)])
                        TT_(pbf[prs, 0:16], tmp[prs, 0, 0:16], pu[prs, 0:16], ALU.subtract, [("tmp", 0)] + tk(3), pk_)
                    for t in range(NT):
                        bank = 6 + t % 2
                        MM(ps[bank][:], wpool_bf[:, l * 2 + c, :], pbf[:, tsl(t)], True, True,
                           [("wpool", l, 2 * c), ("wpool", l, 2 * c + 1)] + pk_, [("ps", bank)])
                        ACT(brBC[:, c, tsl(t)], ps[bank][:], AF.Identity, [("ps", bank), ("vec", l)], [brk(4 + c, t)],
                            scale=vec_sb[:, l, V_PSCALE + c:V_PSCALE + c + 1])
                pieces = []
                for n, col in enumerate((C_CH, C_CB, C_CC, C_PU)):
                    pieces.append((lambda sl, n=n: k8(sl)[:, :, n * 128:(n + 1) * 128],
                                   win[:, :, col + c * 128:col + (c + 1) * 128]))
                return (pieces, fn)

            items.append(p2(0))
            items.append(p2(1))
            if l == 0:
                items.append((None, lambda _: dump("pbf", wbf(72 * K1, 4 * K1), 2048, [("tp", tag, 1, t) for t in range(NT)])))
                items.append((None, lambda _: dump("pu", wf32(88 * K1, 8 * K1), 2048, [("tp", tag, 3, t) for t in range(NT)])))
                items.append((None, lambda _: dump("brBC", wbf(16 * K1, 16 * K1), 8192, [brk(4 + c, t) for c in range(4) for t in range(NT)])))

            def mk_(k, t):
                return ("mg", tag, k, t)

            p4_state = [0]

            def p4(c):
                def fn(s):
                    slot = k8(ring[:, s, :])
                    if c == 0:
                        alloc(64 * K1, 96 * K1, [mk_(k, t) for k in range(KD) for t in range(NT)])
                    br_rng = ((0, 4), (4, 6), (6, 8))
                    for th in range(2):
                        for b in range(3):
                            st_ = p4_state[0] % 2
                            p4_state[0] += 1
                            tiles = (2 * th, 2 * th + 1)
                            for k in range(KD):
                                for tt, t in enumerate(tiles):
                                    gb = 2 * st_ + tt
                                    MM(ps[gb][:], slot[:, k, b * 128:(b + 1) * 128], hT2[:, k, tsl(t)], k == 0, k == KD - 1,
                                       ring_keys(s) + [h2k(k, t)], [("ps", gb)])
                            k0, k1 = br_rng[b]
                            for k in range(k0, k1):
                                for tt, t in enumerate(tiles):
                                    bb = 4 + 2 * st_ + tt
                                    src = brA[:, k, tsl(t)] if k < 4 else brBC[:, k - 4, tsl(t)]
                                    MM(ps[bb][:], slot[:, k, 384:512], src, k == k0, k == k1 - 1,
                                       ring_keys(s) + [brk(k, t)], [("ps", bb)])
                            for tt, t in enumerate(tiles):
                                gb = 2 * st_ + tt
                                bb = 4 + 2 * st_ + tt
                                ti = tt
                                ACT(tmp[:, ti, :], ps[gb][:], AF.Sigmoid, [("ps", gb)], [("tmp", ti)])
                                acc = rstd[:, tsl(t)]
                                if b == 0:
                                    TT_(acc, ps[bb][:], tmp[:, ti, :], ALU.mult, [("ps", bb), ("tmp", ti)],
                                        [("rstd", t)])
                                else:
                                    TT_(tmp[:, ti, :], ps[bb][:], tmp[:, ti, :], ALU.mult,
                                        [("ps", bb), ("tmp", ti)], [("tmp", ti)])
                                    if b == 1:
                                        TT_(acc, acc, tmp[:, ti, :], ALU.add, [("rstd", t), ("tmp", ti)], [("rstd", t)])
                                    else:
                                        TT_(merged[:, c, tsl(t)], acc, tmp[:, ti, :], ALU.add,
                                            [("rstd", t), ("tmp", ti)], [mk_(c, t)])
                pieces = []
                for b in range(3):
                    pieces.append((lambda sl, b=b: k8(sl)[:, :, b * 128:(b + 1) * 128],
                                   win[:, :, C_G + b * D + c * 128:C_G + b * D + (c + 1) * 128]))
                pieces.append((lambda sl: k8(sl)[:, :, 384:512], wbr[:, :, c * 128:(c + 1) * 128]))
                return (pieces, fn)

            for c in range(KD):
                items.append(p4(c))
            if l == 0:
                items.append((None, lambda _: dump("merged", wbf(64 * K1, 32 * K1), 16384, [mk_(k, t) for k in range(KD) for t in range(NT)])))

            def yk(k, t):
                return ("y", tag, k, t)

            def p5(half):
                def fn(s):
                    if half == 0:
                        alloc(0, 64 * K1, [yk(k, t) for k in range(KD) for t in range(NT)])
                    proj_fm(k8(ring[:, s, :]), s, 4, lambda c: c * 128, merged, mk_,
                            lambda c, t, p_, pk: EVAC(y_sb[:, half * 4 + c, tsl(t)], p_, [pk], [yk(half * 4 + c, t)]),
                            None, bsc)
                return ([(k8, wout[:, :, half * 512:(half + 1) * 512])], fn)

            items.append(p5(0))
            items.append(p5(1))

            def p5b(tiles):
                def fn(_):
                    for t in tiles:
                        norm_stats(t, lambda k, t_: y_sb[:, k, tsl(t_)], yk)
                        residual_update(l, t, lambda k, t_=t: y_sb[:, k, tsl(t_)], lambda k, t_=t: yk(k, t_), 8)
                return fn

            defer_p5b = False
            if mod_in_p5b:
                for t in range(NT):
                    items.append((None, p5b((t,))))
                    for it in modl1[3 * t:3 * t + 3]:
                        items.append(it)
            else:
                items.append((None, p5b((0, 1) if defer_p5b else (0, 1, 2, 3))))
            if l == 0:
                items.append((None, lambda _: dump("xmix", x_sb[:].rearrange("p k t -> p (k t)"), 16384, [xk(k, t) for k in range(KD) for t in range(NT)])))
            if stop_after_mix:
                return

            for hf in range(2):
                ftag = "%s_f%d" % (tag, hf)
                h2T = wbf(0, 16 * K1).rearrange("p (k t) -> p k t", k=KD)
                aT = wbf(16 * K1, 64 * K1).rearrange("p (k t) -> p k t", k=32)
                y2lo = wf32(0, 16 * K1).rearrange("p (k t) -> p k t", k=4)
                y2hi = wf32(80 * K1, 16 * K1).rearrange("p (k t) -> p k t", k=4)

                def f1(_, hf=hf, ftag=ftag, h2T=h2T):
                    alloc(0, 16 * K1, [("h2", ftag, k, tl) for k in range(KD) for tl in range(2)])
                    for tl in range(2):
                        t = hf * 2 + tl
                        if hf == 0:
                            norm_stats(t, lambda k, t_: x_sb[:, k, tsl(t_)], xk, split=False)
                        modulate(l, t, lambda k, tl_=tl: h2T[:, k, tl_ * TT:(tl_ + 1) * TT],
                                 lambda k, tl_=tl: ("h2", ftag, k, tl_), 16)

                items.append((None, f1))

                def f2(g, hf=hf, ftag=ftag, h2T=h2T, aT=aT):
                    def fn(s):
                        slot = k8(ring[:, s, :])
                        if g == 0:
                            alloc(16 * K1, 80 * K1, [("a", ftag, kc, tl) for kc in range(32) for tl in range(2)])
                        for cc in range(4):
                            kc = g * 4 + cc
                            for k in range(KD):
                                for tl in range(2):
                                    bank = (kc % 4) * 2 + tl
                                    MM(ps[bank][:], slot[:, k, cc * 128:(cc + 1) * 128], h2T[:, k, tl * TT:(tl + 1) * TT],
                                       k == 0, k == KD - 1, ring_keys(s) + [("h2", ftag, k, tl)], [("ps", bank)])
                            for tl in range(2):
                                bank = (kc % 4) * 2 + tl
                                ti = tl
                                ACT(tmp[:, ti, :], ps[bank][:], AF.Relu, [("ps", bank)], [("tmp", ti)])
                                TT_(aT[:, kc, tl * TT:(tl + 1) * TT], tmp[:, ti, :], tmp[:, ti, :], ALU.mult,
                                    [("tmp", ti)], [("a", ftag, kc, tl)])
                    return ([(k8, wff1[:, :, g * 512:(g + 1) * 512])], fn)

                for g in range(8):
                    items.append(f2(g))
                    if hf == 0 and defer_p5b and g in (0, 1):
                        items.append((None, p5b((2 + g,))))

                def f3(c, hf=hf, ftag=ftag, aT=aT, y2lo=y2lo, y2hi=y2hi):
                    def fn(s):
                        slot = ring[:, s, :].rearrange("p (k n) -> p k n", k=32)
                        if c == 0:
                            alloc(0, 16 * K1, [("y2", ftag, k, tl) for k in range(4) for tl in range(2)])
                            alloc(80 * K1, 96 * K1, [("y2", ftag, k, tl) for k in range(4, 8) for tl in range(2)])
                        for kc in range(32):
                            for tl in range(2):
                                bank = (c % 4) * 2 + tl
                                MM(ps[bank][:], slot[:, kc, :], aT[:, kc, tl * TT:(tl + 1) * TT], kc == 0, kc == 31,
                                   ring_keys(s) + [("a", ftag, kc, tl)], [("ps", bank)])
                        for tl in range(2):
                            bank = (c % 4) * 2 + tl
                            dst = (y2lo if c < 4 else y2hi)[:, c % 4, tl * TT:(tl + 1) * TT]
                            EVAC(dst, ps[bank][:], [("ps", bank)], [("y2", ftag, c, tl)])
                    return ([(lambda sl: sl.rearrange("p (k n) -> p k n", k=32), wff2[:, :, c * 128:(c + 1) * 128])], fn)

                for c in range(KD):
                    items.append(f3(c))
                    if hf == 0 and c == 0:
                        def early_stats(_):
                            for t in (2, 3):
                                norm_stats(t, lambda k, t_: x_sb[:, k, tsl(t_)], xk, split=False)
                        items.append((None, early_stats))

                def f4(_, hf=hf, ftag=ftag, y2lo=y2lo, y2hi=y2hi):
                    def yf(k, tl):
                        return (y2lo if k < 4 else y2hi)[:, k % 4, tl * TT:(tl + 1) * TT]
                    for tl in range(2):
                        t = hf * 2 + tl
                        norm_stats(t, lambda k, t_, tl_=tl: yf(k, tl_), lambda k, t_, tl_=tl: ("y2", ftag, k, tl_))
                        residual_update(l, t, lambda k, tl_=tl: yf(k, tl_), lambda k, tl_=tl: ("y2", ftag, k, tl_), 24)
                        if l == n_layers - 1 and not stop_after_mix:
                            ov = out_d.rearrange("(k p) t -> p k t", p=128)
                            for k in range(KD):
                                DMA("sp", ov[:, k, tsl(t)], x_sb[:, k, tsl(t)], [xk(k, t)], (), s_out[k])

                items.append((None, f4))

        mod0 = mod_item_list(0)
        for it in mod0[0:4]:
            items.append(it)
        for l in range(n_layers):
            layer_items(l)

        def fin(_):
            ov = out_d.rearrange("(k p) t -> p k t", p=128)
            for k in range(KD):
                DMA("sp", ov[:, k, :], x_sb[:, k, :], [xk(k, t) for t in range(NT)], (), s_out[k])

        if stop_after_mix:
            items.append((None, fin))
        run_items()
        P.finalize(nc)
        build_nc.stats = P.stats
        build_nc.dbg_off = dbg_off
    return nc


def _consts():
    cons = np.zeros((128, NCON_D), np.float32)
    s = np.arange(128)
    cons[:, CON_MASK:CON_MASK + 128] = (s[None, :] >= s[:, None]).astype(np.float32)
    cons[:, CON_ID:CON_ID + 128] = np.eye(128, dtype=np.float32)
    cons[:, CON_MNEG:CON_MNEG + 128] = np.where(s[None, :] < s[:, None], MASK_NEG, 0.0).astype(np.float32)
    wins = (2, 4, 8, 16)
    for c in range(2):
        for half in range(2):
            w = wins[c * 2 + half]
            rows = slice(half * 64, half * 64 + 64)
            cons[rows, CON_INVW + c] = 1.0 / w
            cons[rows, CON_INVC + c * 16:CON_INVC + (c + 1) * 16] = 1.0 / np.minimum(np.arange(1, 17), w)
    return cons


def _fm(v):
    return np.ascontiguousarray(np.asarray(v, np.float32).reshape(-1, 128).T)


_NC_CACHE = {}


def kernel(x, c, w_ada, b_ada, g_mix_pre, g_mix_post, g_ff_pre, g_ff_post, w_in, b_f,
           w_pool, pool_scale, conv_w, w_branch, w_out, w_ff1, w_ff2):
    x = np.asarray(x, np.float32)
    n = x.shape[0]
    vecs = np.zeros((DEPTH, 128, NV), np.float32)
    for l in range(DEPTH):
        vecs[l, :, V_BADA:V_BADA + 48] = _fm(b_ada[l])
        vecs[l, :, V_GMPRE:V_GMPRE + 8] = _fm(g_mix_pre[l])
        vecs[l, :, V_GMPOST:V_GMPOST + 8] = _fm(g_mix_post[l])
        vecs[l, :, V_GFPRE:V_GFPRE + 8] = _fm(g_ff_pre[l])
        vecs[l, :, V_GFPOST:V_GFPOST + 8] = _fm(g_ff_post[l])
        vecs[l, :, V_PSCALE:V_PSCALE + 2] = _fm(pool_scale[l])
        for tap in range(3):
            vecs[l, :, V_CONVW + tap * 2:V_CONVW + tap * 2 + 2] = _fm(conv_w[l, tap])
        vecs[l, :, V_BF:V_BF + 128] = np.tile(np.asarray(b_f[l], np.float32), 16)[None, :]
    cons = _consts()
    shared = {
        "vecs": vecs, "cons": cons,
        "w_ada": np.ascontiguousarray(w_ada, np.float32), "w_in": np.ascontiguousarray(w_in, np.float32),
        "w_pool": np.ascontiguousarray(w_pool, np.float32), "w_branch": np.ascontiguousarray(w_branch, np.float32),
        "w_out": np.ascontiguousarray(w_out, np.float32), "w_ff1": np.ascontiguousarray(w_ff1, np.float32),
        "w_ff2": np.ascontiguousarray(w_ff2, np.float32),
    }
    in_maps = []
    for b in range(n):
        m = dict(shared)
        m["xT"] = np.ascontiguousarray(x[b].T)
        m["cT"] = _fm(c[b])
        in_maps.append(m)
    if "nc" not in _NC_CACHE:
        _NC_CACHE["nc"] = build_nc()
    nc = _NC_CACHE["nc"]
    res = run_bass_kernel_spmd(nc, in_maps, core_ids=list(range(n)))
    out = np.stack([np.ascontiguousarray(r["outT"].T) for r in res.results], axis=0)
    return out.astype(np.float32)
```
